# Optimizing a Trainium2 kernel written in Bass

```python
import math
import jax, jax.numpy as jnp
from jax import lax
import numpy as np

D_MODEL = 2048
BATCH = 8
SEQ = 2048
DEPTH = 2

N_MIXERS = 2
N_CONF = (DEPTH + 1) // 2
N_HYENA = DEPTH // 2
D_FF = ((8 * D_MODEL // 3 + 255) // 256) * 256
CONV_WIDTH = 31
HYENA_SHORT_WIDTH = 3
HYENA_N_BANDS = 16
HYENA_EMB_DIM = 1 + 2 * HYENA_N_BANDS
HYENA_FILTER_ORDER = 64
HYENA_FAST_DECAY_PCT = 0.3
HYENA_SLOW_DECAY_PCT = 1.5
HYENA_DECAY_TARGET = 1e-2
NORM_EPS = 1e-6
LN_EPS = 1e-5

kernel_name = "hybrid_conformer_hyena_encoder"


def rmsnorm(x, g):
    xf = x.astype(jnp.float32)
    y = xf * lax.rsqrt(jnp.mean(xf * xf, axis=-1, keepdims=True) + NORM_EPS)
    return (y * g.astype(jnp.float32)).astype(x.dtype)


def layernorm(x, g, b):
    xf = x.astype(jnp.float32)
    mu = jnp.mean(xf, axis=-1, keepdims=True)
    var = jnp.mean(jnp.square(xf - mu), axis=-1, keepdims=True)
    y = (xf - mu) * lax.rsqrt(var + LN_EPS)
    return (y * g.astype(jnp.float32) + b.astype(jnp.float32)).astype(x.dtype)


def depthwise_conv_centred(x, w):
    k = w.shape[0]
    pad = (k - 1) // 2
    return lax.conv_general_dilated(
        x, w[:, None, :].astype(x.dtype), window_strides=(1,),
        padding=[(pad, pad)], dimension_numbers=("NWC", "WIO", "NWC"),
        feature_group_count=x.shape[-1])


def conformer_conv(h, w_pw1, b_pw1, w_dw, b_dw, ln_g, ln_b, w_pw2, b_pw2):
    a = h @ w_pw1 + b_pw1
    u = jax.nn.glu(a, axis=-1)
    u = depthwise_conv_centred(u, w_dw) + b_dw
    u = jax.nn.silu(layernorm(u, ln_g, ln_b))
    return u @ w_pw2 + b_pw2


def hyena_filters(L, w1, b1, f1, w2, b2, f2, w3, b3, f3, w4):
    f32 = jnp.float32
    t = jnp.linspace(0.0, 1.0, L, dtype=f32)[:, None]
    bands = jnp.linspace(1e-4, HYENA_N_BANDS - 1, HYENA_N_BANDS, dtype=f32)[None, :]
    wpos = (2.0 * math.pi) * jnp.arange(L, dtype=f32)[:, None] / L
    z = jnp.concatenate([t, jnp.cos(bands * wpos), -jnp.sin(bands * wpos)], axis=-1)
    hf = jnp.sin(f1.astype(f32) * (z @ w1.astype(f32) + b1.astype(f32)))
    hf = jnp.sin(f2.astype(f32) * (hf @ w2.astype(f32) + b2.astype(f32)))
    hf = jnp.sin(f3.astype(f32) * (hf @ w3.astype(f32) + b3.astype(f32)))
    k = hf @ w4.astype(f32)
    max_decay = math.log(HYENA_DECAY_TARGET) / HYENA_FAST_DECAY_PCT
    min_decay = math.log(HYENA_DECAY_TARGET) / HYENA_SLOW_DECAY_PCT
    deltas = jnp.abs(jnp.linspace(min_decay, max_decay, D_MODEL, dtype=f32))[None, :]
    decay = jnp.exp(-t * deltas)
    return k * jnp.concatenate([decay, decay], axis=-1)


def bidirectional_fftconv(v, k, skip):
    L = v.shape[1]
    kf, kb = k[:, :D_MODEL], k[:, D_MODEL:]
    kern = jnp.concatenate([kf, jnp.zeros((1, D_MODEL), jnp.float32), kb[1:][::-1]], axis=0)
    kfreq = jnp.fft.rfft(kern, axis=0)
    vf = v.astype(jnp.float32)
    vfreq = jnp.fft.rfft(vf, n=2 * L, axis=1)
    y = jnp.fft.irfft(vfreq * kfreq[None], n=2 * L, axis=1)[:, :L]
    y = y + vf * skip.astype(jnp.float32)
    return y.astype(v.dtype)


def hyena(h, w_in, b_in, w_short, b_short, w1, b1, f1, w2, b2, f2, w3, b3, f3, w4,
          skip, w_out, b_out):
    L = h.shape[1]
    zc = h @ w_in + b_in
    zc = depthwise_conv_centred(zc, w_short) + b_short
    x0, x1, v = jnp.split(zc, 3, axis=-1)
    k = hyena_filters(L, w1, b1, f1, w2, b2, f2, w3, b3, f3, w4)
    v = bidirectional_fftconv(v * x1, k, skip)
    return (v * x0) @ w_out + b_out


def swiglu(h, w_gate, w_up, w_down):
    return (jax.nn.silu(h @ w_gate) * (h @ w_up)) @ w_down


def setup_inputs(seed: int = 0) -> dict:
    key = jax.random.key(seed)
    ks = jax.random.split(key, 40)
    D, F = D_MODEL, D_FF
    f32 = jnp.float32
    nrm = lambda k, s, sc: jax.random.normal(k, s, f32) * sc
    gain = lambda k, s: 1.0 + 0.01 * jax.random.normal(k, s, f32)
    return {
        "x": jax.random.normal(ks[0], (BATCH, SEQ, D), f32),
        "norm_mix": gain(ks[1], (DEPTH, D)),
        "norm_ffn": gain(ks[2], (DEPTH, D)),
        "cv_w_pw1": nrm(ks[3], (N_CONF, D, 2 * D), D ** -0.5),
        "cv_b_pw1": nrm(ks[4], (N_CONF, 2 * D), 0.01),
        "cv_w_dw": nrm(ks[5], (N_CONF, CONV_WIDTH, D), CONV_WIDTH ** -0.5),
        "cv_b_dw": nrm(ks[6], (N_CONF, D), 0.01),
        "cv_ln_g": gain(ks[7], (N_CONF, D)),
        "cv_ln_b": nrm(ks[8], (N_CONF, D), 0.01),
        "cv_w_pw2": nrm(ks[9], (N_CONF, D, D), D ** -0.5),
        "cv_b_pw2": nrm(ks[10], (N_CONF, D), 0.01),
        "hy_w_in": nrm(ks[11], (N_HYENA, D, 3 * D), D ** -0.5),
        "hy_b_in": nrm(ks[12], (N_HYENA, 3 * D), 0.01),
        "hy_w_short": nrm(ks[13], (N_HYENA, HYENA_SHORT_WIDTH, 3 * D), HYENA_SHORT_WIDTH ** -0.5),
        "hy_b_short": nrm(ks[14], (N_HYENA, 3 * D), 0.01),
        "hy_f_w1": nrm(ks[15], (N_HYENA, HYENA_EMB_DIM, HYENA_FILTER_ORDER), HYENA_EMB_DIM ** -0.5),
        "hy_f_b1": nrm(ks[16], (N_HYENA, HYENA_FILTER_ORDER), 0.01),
        "hy_f_freq1": gain(ks[17], (N_HYENA, HYENA_FILTER_ORDER)),
        "hy_f_w2": nrm(ks[18], (N_HYENA, HYENA_FILTER_ORDER, HYENA_FILTER_ORDER), HYENA_FILTER_ORDER ** -0.5),
        "hy_f_b2": nrm(ks[19], (N_HYENA, HYENA_FILTER_ORDER), 0.01),
        "hy_f_freq2": gain(ks[20], (N_HYENA, HYENA_FILTER_ORDER)),
        "hy_f_w3": nrm(ks[21], (N_HYENA, HYENA_FILTER_ORDER, HYENA_FILTER_ORDER), HYENA_FILTER_ORDER ** -0.5),
        "hy_f_b3": nrm(ks[22], (N_HYENA, HYENA_FILTER_ORDER), 0.01),
        "hy_f_freq3": gain(ks[23], (N_HYENA, HYENA_FILTER_ORDER)),
        "hy_f_w4": nrm(ks[24], (N_HYENA, HYENA_FILTER_ORDER, 2 * D), 0.002),
        "hy_skip": nrm(ks[25], (N_HYENA, D), 1.0),
        "hy_w_out": nrm(ks[26], (N_HYENA, D, D), D ** -0.5),
        "hy_b_out": nrm(ks[27], (N_HYENA, D), 0.01),
        "ffn_w_gate": nrm(ks[28], (DEPTH, D, F), D ** -0.5),
        "ffn_w_up": nrm(ks[29], (DEPTH, D, F), D ** -0.5),
        "ffn_w_down": nrm(ks[30], (DEPTH, F, D), F ** -0.5),
        "norm_final": gain(ks[31], (D,)),
    }


def reference(x, norm_mix, norm_ffn,
              cv_w_pw1, cv_b_pw1, cv_w_dw, cv_b_dw, cv_ln_g, cv_ln_b, cv_w_pw2, cv_b_pw2,
              hy_w_in, hy_b_in, hy_w_short, hy_b_short,
              hy_f_w1, hy_f_b1, hy_f_freq1, hy_f_w2, hy_f_b2, hy_f_freq2,
              hy_f_w3, hy_f_b3, hy_f_freq3, hy_f_w4, hy_skip, hy_w_out, hy_b_out,
              ffn_w_gate, ffn_w_up, ffn_w_down, norm_final):
    h = x
    for i in range(DEPTH):
        hn = rmsnorm(h, norm_mix[i])
        j = i // N_MIXERS
        if i % N_MIXERS == 0:
            mix = conformer_conv(hn, cv_w_pw1[j], cv_b_pw1[j], cv_w_dw[j], cv_b_dw[j],
                                 cv_ln_g[j], cv_ln_b[j], cv_w_pw2[j], cv_b_pw2[j])
        else:
            mix = hyena(hn, hy_w_in[j], hy_b_in[j], hy_w_short[j], hy_b_short[j],
                        hy_f_w1[j], hy_f_b1[j], hy_f_freq1[j],
                        hy_f_w2[j], hy_f_b2[j], hy_f_freq2[j],
                        hy_f_w3[j], hy_f_b3[j], hy_f_freq3[j], hy_f_w4[j],
                        hy_skip[j], hy_w_out[j], hy_b_out[j])
        h = h + mix
        hn = rmsnorm(h, norm_ffn[i])
        h = h + swiglu(hn, ffn_w_gate[i], ffn_w_up[i], ffn_w_down[i])
    return rmsnorm(h, norm_final)
```

```python
import math
import numpy as np
import ml_dtypes
import concourse.bass as bass
import concourse.mybir as mybir
from concourse.bass_utils import run_bass_kernel_spmd

F32 = mybir.dt.float32
BF16 = mybir.dt.bfloat16
AF = mybir.ActivationFunctionType
ALU = mybir.AluOpType
NPBF = ml_dtypes.bfloat16

D = 2048
L = 2048
FF = 5632
NCH = 16
NORM_EPS = 1e-6
LN_EPS = 1e-5
TWO_PI = 2.0 * math.pi

PCOL = {}


def _mkcols():
    off = 0
    for name, n in [("nm0", 16), ("nf0", 16), ("nm1", 16), ("nf1", 16), ("nfin", 16),
                    ("b_pw1", 32), ("b_dw", 16), ("ln_g", 16), ("ln_b", 16), ("b_pw2", 16),
                    ("b_in", 48), ("b_sh", 48), ("ws0", 48), ("ws1", 48), ("ws2", 48),
                    ("skip", 16), ("b_out", 16), ("wdw", 496), ("negt", 16)]:
        PCOL[name] = off
        off += n
    return off


NPV = _mkcols()


class Buf:
    __slots__ = ("w", "r")

    def __init__(self):
        self.w = None
        self.r = {}


class Eng:
    def __init__(self, nc, eng, name, is_pe=False):
        self.e = eng
        self.sem = nc.alloc_semaphore("se_" + name)
        self.cnt = 0
        self.seen = {}
        self.is_pe = is_pe


class DSem:
    def __init__(self, nc, name):
        self.sem = nc.alloc_semaphore("sd_" + name)
        self.cnt = 0


class Slot:
    def __init__(self, nc, name, shape, dtype, dsem=True):
        self.t = nc.alloc_sbuf_tensor(name, shape, dtype)
        self.b = Buf()
        self.d = DSem(nc, name) if dsem else None


class KB:
    def __init__(self, nc):
        self.nc = nc
        self.PE = Eng(nc, nc.tensor, "pe", True)
        self.ACT = Eng(nc, nc.scalar, "act")
        self.DVE = Eng(nc, nc.vector, "dve")
        self.POOL = Eng(nc, nc.gpsimd, "pool")
        self.SP = Eng(nc, nc.sync, "sp")
        self.engs = [self.PE, self.ACT, self.DVE, self.POOL, self.SP]
        self.dsems = []

    def wait(self, E, ev):
        sem, val = ev
        if E.is_pe and sem is E.sem:
            return
        if E.seen.get(sem.num, 0) >= val:
            return
        E.e.wait_ge(sem, val)
        E.seen[sem.num] = val

    def _deps(self, E, reads, writes):
        for b in reads:
            if b.w is not None:
                self.wait(E, b.w)
        for b in writes:
            if b.w is not None:
                self.wait(E, b.w)
            for ev in b.r.values():
                self.wait(E, ev)

    def _commit(self, ev, reads, writes):
        for b in reads:
            b.r[ev[0].num] = ev
        for b in writes:
            b.w = ev
            b.r = {}

    def op(self, E, fn, reads=(), writes=()):
        self._deps(E, reads, writes)
        ins = fn()
        E.cnt += 1
        ins.then_inc(E.sem, 1)
        self._commit((E.sem, E.cnt), reads, writes)

    def dma(self, Q, ds, out, in_, reads=(), writes=(), **kw):
        self._deps(Q, reads, writes)
        if ds.cnt > 0:
            self.wait(Q, (ds.sem, ds.cnt))
        ins = Q.e.dma_start(out=out, in_=in_, **kw)
        ds.cnt += 16
        ins.then_inc(ds.sem, 16)
        self._commit((ds.sem, ds.cnt), reads, writes)

    def new_dsem(self, name):
        d = DSem(self.nc, name)
        self.dsems.append(d)
        return d

    def barrier(self):
        for E in self.engs:
            for E2 in self.engs:
                if E2 is not E and E2.cnt > 0:
                    self.wait(E, (E2.sem, E2.cnt))
            for d in self.dsems:
                if d.cnt > 0:
                    self.wait(E, (d.sem, d.cnt))

    def finish(self):
        E = self.SP
        for E2 in self.engs:
            if E2 is not E and E2.cnt > 0:
                self.wait(E, (E2.sem, E2.cnt))
        for d in self.dsems:
            if d.cnt > 0:
                self.wait(E, (d.sem, d.cnt))


class Prog:
    def __init__(self, phases, dbg=False):
        nc = bass.Bass("TRN2", target_bir_lowering=False)
        self.nc = nc
        self.kb = KB(nc)
        kb = self.kb
        self.phases = phases

        def din(name, shape, dt=F32):
            return nc.dram_tensor(name, shape, dt, kind="ExternalInput").ap()

        self.d_x = din("xT", [16, 128, L])
        self.d_pv = din("pv", [128, NPV])
        self.d_pw1 = din("w_pw1", [32, 128, 2048])
        self.d_pw2 = din("w_pw2", [16, 128, 2048])
        self.d_wg = din("w_gate", [2, 44, 128, 2048])
        self.d_wu = din("w_up", [2, 44, 128, 2048])
        self.d_wd = din("w_down", [2, 4, 16, 128, 1408])
        self.d_win = din("w_in", [48, 128, 2048])
        self.d_wout = din("w_out", [16, 128, 2048])
        self.d_fw1 = din("fw1", [33, 64])
        self.d_fw2 = din("fw2", [64, 64])
        self.d_fw3 = din("fw3", [64, 64])
        self.d_fw4 = din("fw4", [64, 4096])
        self.d_fvec = din("fvec", [64, 6])
        self.d_zT = din("zT", [33, L])
        self.d_delta = din("delta", [128, D])
        self.d_cf = din("cf", [32, 128, 2048], BF16)
        self.d_cfh = din("cfh", [8, 2, 128, 2048], BF16)
        self.d_ci = din("ci", [2, 2, 16, 128, 512], BF16)
        self.d_ident = din("ident", [128, 128], BF16)
        self.d_ones = din("ones", [128, 128], BF16)
        hk = "ExternalInput" if ("hin" in phases) else "Internal"
        self.d_h = nc.dram_tensor("h", [16, 128, L], F32, kind=hk).ap()
        self.d_kf = nc.dram_tensor("kf", [4, 8, 128, 2048], BF16,
                                   kind=("ExternalInput" if "kfin" in phases else "Internal")).ap()
        self.d_out = nc.dram_tensor("outT", [16, 128, L], F32, kind="ExternalOutput").ap()
        self.d_g = nc.dram_tensor("gscr", [16, 128, L], BF16).ap()
        self.gb = [Buf() for _ in range(4)]
        self.d_kfo = None
        if "kfout" in phases:
            self.d_kfo = nc.dram_tensor("kfo", [4, 8, 128, 2048], BF16, kind="ExternalOutput").ap()

        self.hb = [Buf() for _ in range(16)]
        self.kfb = [[Buf() for _ in range(16)] for _ in range(4)]

        self.arena = nc.alloc_sbuf_tensor("arena", [128, 65536], BF16)
        self.arena_b = Buf()
        self.HN = self.arena[:, 0:32768].rearrange("p (c t) -> p c t", c=16)
        self.HNb = Buf()
        self.RB = self.arena[:, 32768:65536]
        self.RBb = Buf()
        self.FS = [Slot(nc, f"fs{i}", [128, 2050], F32) for i in range(4)]
        self.W = [Slot(nc, f"w{i}", [128, 2048], BF16) for i in range(4)]
        self.SS = [Slot(nc, f"ss{i}", [128, 512], F32, dsem=False) for i in range(8)]
        self.KFT = [Slot(nc, f"kft{i}", [128, 2048], BF16) for i in range(2)]
        self.PV = Slot(nc, "pvt", [128, NPV], F32)
        self.IDENT = Slot(nc, "identt", [128, 128], BF16)
        self.ONES = Slot(nc, "onest", [128, 128], BF16)
        self.FV = Slot(nc, "fvect", [64, 8], F32)
        self.B2 = Slot(nc, "b2t", [128, 48], F32, dsem=False)
        self.FW = Slot(nc, "fwt", [64, 192], F32)
        for s in self.FS + self.W + self.KFT + [self.PV, self.IDENT, self.ONES, self.FV, self.FW]:
            kb.dsems.append(s.d)
        self.bank = []
        for i in range(8):
            s = Slot.__new__(Slot)
            s.t = nc.alloc_psum_tensor(f"bank{i}", [128, 512], F32)
            s.b = Buf()
            s.d = None
            self.bank.append(s)
        self.wi = 0
        self.fsi = 0
        self.ssi = 0
        self.kfi = 0

        self.emit()

    def pcol(self, name, c, n=128):
        return self.PV.t[0:n, PCOL[name] + c:PCOL[name] + c + 1]

    def load_w(self, src, ncols, cast=True):
        kb = self.kb
        s = self.W[self.wi % len(self.W)]
        self.wi += 1
        Q = kb.POOL if cast else kb.SP
        kb.dma(Q, s.d, s.t[:, 0:ncols], src, writes=[s.b])
        return s

    def next_fs(self):
        s = self.FS[self.fsi % len(self.FS)]
        self.fsi += 1
        return s

    def next_ss(self):
        s = self.SS[self.ssi % len(self.SS)]
        self.ssi += 1
        return s

    def emit(self):
        kb = self.kb
        nc = self.nc
        ph = self.phases
        kb.dma(kb.SP, self.PV.d, self.PV.t[:], self.d_pv, writes=[self.PV.b])
        kb.dma(kb.SP, self.IDENT.d, self.IDENT.t[:], self.d_ident, writes=[self.IDENT.b])
        kb.dma(kb.SP, self.ONES.d, self.ONES.t[:], self.d_ones, writes=[self.ONES.b])
        xsrc = [(self.d_x[c], None) for c in range(16)]
        hch = [(self.d_h[c], self.hb[c]) for c in range(16)]
        och = [(self.d_out[c], Buf()) for c in range(16)]
        if "filter" in ph:
            self.filter_phase()
            kb.barrier()
        if "conf" in ph:
            self.rmsnorm(xsrc, "nm0", None)
            self.conformer(xsrc, hch)
            kb.barrier()
        if "ffn0" in ph:
            self.rmsnorm(hch, "nf0", None, have_stats=("conf" in ph))
            self.ffn(0, hch)
            kb.barrier()
        if "hyena" in ph:
            self.rmsnorm(hch, "nm1", None, have_stats=("ffn0" in ph))
            self.hyena(hch)
            kb.barrier()
        if "ffn1" in ph:
            self.rmsnorm(hch, "nf1", None, have_stats=("hyena" in ph))
            self.ffn(1, hch)
            kb.barrier()
        if "final" in ph:
            self.rmsnorm(hch, "nfin", och, have_stats=("ffn1" in ph))
        else:
            for c in range(16):
                s = self.next_fs()
                kb.dma(kb.SP, s.d, s.t[:, 0:L], hch[c][0], reads=[hch[c][1]], writes=[s.b])
                kb.dma(kb.SP, s.d, och[c][0], s.t[:, 0:L], reads=[s.b], writes=[och[c][1]])
        if self.d_kfo is not None:
            for q in range(4):
                for fc in range(8):
                    s = self.KFT[(q * 8 + fc) % 2]
                    kb.dma(kb.SP, s.d, s.t[:], self.d_kf[q, fc], reads=[self.kfb[q][fc]], writes=[s.b])
                    kb.dma(kb.SP, s.d, self.d_kfo[q, fc], s.t[:], reads=[s.b])
        kb.finish()

    def rmsnorm(self, src, gname, dst, have_stats=False):
        kb = self.kb
        nc = self.nc
        sb = 4 if have_stats else 0
        SQ = [(self.FS[3].t[:, 0:1024].bitcast(BF16), self.FS[3].b)]
        if not have_stats:
            for c in range(16):
                ap, b = src[c]
                xs = self.FS[c % 3]
                kb.dma(kb.SP, xs.d, xs.t[:, 0:L], ap, reads=([b] if b else []), writes=[xs.b])
                sq, sqb = SQ[0]
                kb.op(kb.ACT, lambda: nc.scalar.activation(out=sq, in_=xs.t[:, 0:L], func=AF.Square),
                      reads=[xs.b], writes=[sqb])

                def mm():
                    for nb in range(4):
                        ins = nc.tensor.matmul(self.bank[nb].t[:], self.ONES.t[:], sq[:, nb * 512:(nb + 1) * 512],
                                               start=(c == 0), stop=(c == 15))
                    return ins
                kb.op(kb.PE, mm, reads=[sqb, self.ONES.b], writes=[self.bank[nb].b for nb in range(4)])
        for nb in range(4):
            bk = self.bank[sb + nb]
            tmp = self.next_ss()
            kb.op(kb.ACT, lambda: nc.scalar.activation(out=tmp.t[:], in_=bk.t[:], func=AF.Sqrt,
                                                       bias=self.EPSN[:, 0:1], scale=1.0 / D),
                  reads=[bk.b], writes=[tmp.b])
            kb.op(kb.DVE, lambda: nc.vector.reciprocal(out=bk.t[:], in_=tmp.t[:]),
                  reads=[tmp.b], writes=[bk.b])
        def ld2(m):
            ap, b = src[m]
            x_ = self.FS[m % 3]
            kb.dma(kb.SP, x_.d, x_.t[:, 0:L], ap, reads=([b] if b else []), writes=[x_.b])
        ld2(0)
        for c in range(16):
            xs = self.FS[c % 3]
            if c + 1 < 16:
                ld2(c + 1)
            for nb in range(4):
                sl = slice(nb * 512, (nb + 1) * 512)
                bk = self.bank[sb + nb]
                if dst is None:
                    kb.op(kb.DVE, lambda: nc.vector.scalar_tensor_tensor(
                        out=self.HN[:, c, sl], in0=xs.t[:, sl], scalar=self.pcol(gname, c), in1=bk.t[:],
                        op0=ALU.mult, op1=ALU.mult), reads=[xs.b, bk.b, self.PV.b], writes=[self.HNb])
                else:
                    kb.op(kb.DVE, lambda: nc.vector.scalar_tensor_tensor(
                        out=xs.t[:, sl], in0=xs.t[:, sl], scalar=self.pcol(gname, c), in1=bk.t[:],
                        op0=ALU.mult, op1=ALU.mult), reads=[bk.b, self.PV.b], writes=[xs.b])
            if dst is not None:
                kb.dma(kb.SP, xs.d, dst[c][0], xs.t[:, 0:L], reads=[xs.b], writes=[dst[c][1]])

    def linear_rmw(self, act, actb, KC, wt, bias, src, dst, stats=False):
        kb = self.kb
        nc = self.nc
        SQ = self.FS[3].t[:, 0:1024].bitcast(BF16)
        SQb = self.FS[3].b
        pend = None

        def do_stats(mbp, xsp):
            kb.op(kb.ACT, lambda: nc.scalar.activation(out=SQ, in_=xsp.t[:, 0:L], func=AF.Square),
                  reads=[xsp.b], writes=[SQb])

            def mms():
                for nb in range(4):
                    ins = nc.tensor.matmul(self.bank[4 + nb].t[:], self.ONES.t[:], SQ[:, nb * 512:(nb + 1) * 512],
                                           start=(mbp == 0), stop=(mbp == 15))
                return ins
            kb.op(kb.PE, mms, reads=[SQb, self.ONES.b], writes=[self.bank[4 + nb].b for nb in range(4)])

        def ld(m):
            ap, b = src[m]
            x_ = self.FS[m % 3]
            kb.dma(kb.SP, x_.d, x_.t[:, 0:L], ap, reads=([b] if b else []), writes=[x_.b])
        ld(0)
        for mb in range(16):
            w = self.load_w(wt[mb], KC * 128)
            xs = self.FS[mb % 3]
            if mb + 1 < 16:
                ld(mb + 1)
            ngrp = 2 if stats else 1
            npg = 4 // ngrp
            for gi in range(ngrp):
                if stats:
                    bl = [self.bank[gi * 2 + k] for k in range(2)]
                else:
                    bl = [self.bank[(mb % 2) * 4 + k] for k in range(4)]

                def mm():
                    for kc in range(KC):
                        for k in range(npg):
                            nb = gi * npg + k
                            ins = nc.tensor.matmul(bl[k].t[:], w.t[:, kc * 128:(kc + 1) * 128],
                                                   act[:, kc, nb * 512:(nb + 1) * 512],
                                                   start=(kc == 0), stop=(kc == KC - 1))
                    return ins
                kb.op(kb.PE, mm, reads=[w.b, actb], writes=[bk.b for bk in bl])
                if stats and gi == 0 and pend is not None:
                    do_stats(*pend)
                    pend = None
                for k in range(npg):
                    nb = gi * npg + k
                    sl = slice(nb * 512, (nb + 1) * 512)
                    if bias is not None:
                        kb.op(kb.DVE, lambda: nc.vector.scalar_tensor_tensor(
                            out=xs.t[:, sl], in0=bl[k].t[:], scalar=self.pcol(bias, mb), in1=xs.t[:, sl],
                            op0=ALU.add, op1=ALU.add), reads=[bl[k].b, self.PV.b], writes=[xs.b])
                    else:
                        kb.op(kb.DVE, lambda: nc.vector.tensor_tensor(
                            out=xs.t[:, sl], in0=bl[k].t[:], in1=xs.t[:, sl], op=ALU.add),
                            reads=[bl[k].b], writes=[xs.b])
            kb.dma(kb.SP, xs.d, dst[mb][0], xs.t[:, 0:L], reads=[xs.b], writes=[dst[mb][1]])
            if stats:
                pend = (mb, xs)
        if pend is not None:
            do_stats(*pend)

    def conformer(self, xsrc, hdst):
        kb = self.kb
        nc = self.nc
        C = self.RB.rearrange("p (c t) -> p c t", c=16)
        Cb = self.RBb
        DG = self.FS[0].t[:, 0:1984].bitcast(BF16).rearrange("p (j q) -> p j q", j=31)
        DGb = self.FS[0].b
        U = [(self.FS[1 + i].t[:, 0:1039].bitcast(BF16), self.FS[1 + i].b) for i in range(2)]
        for u, ub in U:
            kb.op(kb.DVE, lambda: nc.vector.memset(u, 0.0), writes=[ub])
        prev = None
        for c in range(17):
            if prev is not None:
                def dgb():
                    for j in range(31):
                        ins = nc.vector.tensor_scalar(out=DG[:, j, :], in0=self.IDENT.t[:],
                                                      scalar1=self.pcol("wdw", prev[0] * 31 + j), scalar2=None,
                                                      op0=ALU.mult)
                    return ins
                kb.op(kb.DVE, dgb, reads=[self.IDENT.b, self.PV.b], writes=[DGb])
            if c < 16:
                wv = self.load_w(self.d_pw1[c], 2048)
                wg = self.load_w(self.d_pw1[16 + c], 2048)
                u, ub = U[c % 2]
                for half in range(2):
                    bs = 4 * half

                    def mm():
                        for kc in range(16):
                            for wt, boff in ((wv, 0), (wg, 2)):
                                for nb in range(2):
                                    t0 = half * 1024 + nb * 512
                                    ins = nc.tensor.matmul(self.bank[bs + boff + nb].t[:],
                                                           wt.t[:, kc * 128:(kc + 1) * 128],
                                                           self.HN[:, kc, t0:t0 + 512],
                                                           start=(kc == 0), stop=(kc == 15))
                        return ins
                    kb.op(kb.PE, mm, reads=[wv.b, wg.b, self.HNb], writes=[self.bank[bs + i].b for i in range(4)])
                    for nb in range(2):
                        t0 = 15 + half * 1024 + nb * 512
                        sg = self.next_ss()
                        gb = self.bank[bs + 2 + nb]
                        vb = self.bank[bs + nb]
                        kb.op(kb.ACT, lambda: nc.scalar.activation(out=sg.t[:], in_=gb.t[:], func=AF.Sigmoid,
                                                                   bias=self.pcol("b_pw1", 16 + c), scale=1.0),
                              reads=[gb.b, self.PV.b], writes=[sg.b])
                        kb.op(kb.DVE, lambda: nc.vector.scalar_tensor_tensor(
                            out=u[:, t0:t0 + 512], in0=vb.t[:], scalar=self.pcol("b_pw1", c), in1=sg.t[:],
                            op0=ALU.add, op1=ALU.mult), reads=[vb.b, sg.b, self.PV.b], writes=[ub])
            if prev is not None:
                pc, pu, pub = prev
                for nb in range(4):
                    bk = self.bank[nb]

                    def mmc():
                        for j in range(31):
                            ins = nc.tensor.matmul(bk.t[:], DG[:, j, :], pu[:, nb * 512 + j:nb * 512 + j + 512],
                                                   start=(j == 0), stop=(j == 30))
                        return ins
                    kb.op(kb.PE, mmc, reads=[DGb, pub], writes=[bk.b])
                    kb.op(kb.ACT, lambda: nc.scalar.activation(out=C[:, pc, nb * 512:(nb + 1) * 512], in_=bk.t[:],
                                                               func=AF.Identity, bias=self.pcol("b_dw", pc), scale=1.0),
                          reads=[bk.b, self.PV.b], writes=[Cb])
            if c < 16:
                prev = (c, u, ub)
        kb.barrier()
        SQS = [(self.FS[3].t[:, 0:1024].bitcast(BF16), self.FS[3].b),
               (self.FS[2].t[:, 0:1024].bitcast(BF16), self.FS[2].b)]
        for c in range(16):
            SQ, SQb = SQS[c % 2]
            kb.op(kb.ACT, lambda: nc.scalar.activation(out=SQ, in_=C[:, c, :], func=AF.Square),
                  reads=[Cb], writes=[SQb])

            def mm():
                for nb in range(4):
                    nc.tensor.matmul(self.bank[nb].t[:], self.ONES.t[:], C[:, c, nb * 512:(nb + 1) * 512],
                                     start=(c == 0), stop=(c == 15))
                    ins = nc.tensor.matmul(self.bank[4 + nb].t[:], self.ONES.t[:], SQ[:, nb * 512:(nb + 1) * 512],
                                           start=(c == 0), stop=(c == 15))
                return ins
            kb.op(kb.PE, mm, reads=[Cb, SQb, self.ONES.b], writes=[bk.b for bk in self.bank])
        T1, T2 = self.FS[2], self.FS[3]
        for nb in range(4):
            bm, bs_ = self.bank[nb], self.bank[4 + nb]
            kb.op(kb.ACT, lambda: nc.scalar.activation(out=bm.t[:], in_=bm.t[:], func=AF.Copy, scale=1.0 / D),
                  writes=[bm.b])
            msq = self.next_ss()
            kb.op(kb.ACT, lambda: nc.scalar.activation(out=msq.t[:], in_=bm.t[:], func=AF.Square),
                  reads=[bm.b], writes=[msq.b])
            var = self.next_ss()
            kb.op(kb.DVE, lambda: nc.vector.scalar_tensor_tensor(
                out=var.t[:], in0=bs_.t[:], scalar=1.0 / D, in1=msq.t[:],
                op0=ALU.mult, op1=ALU.subtract), reads=[bs_.b, msq.b], writes=[var.b])
            kb.op(kb.ACT, lambda: nc.scalar.activation(out=var.t[:], in_=var.t[:], func=AF.Sqrt,
                                                       bias=self.EPSL[:, 0:1], scale=1.0), writes=[var.b])
            kb.op(kb.DVE, lambda: nc.vector.reciprocal(out=bs_.t[:], in_=var.t[:]), reads=[var.b], writes=[bs_.b])
        for c in range(16):
            T = T1 if c % 2 == 0 else T2
            for nb in range(4):
                sl = slice(nb * 512, (nb + 1) * 512)
                kb.op(kb.DVE, lambda: nc.vector.tensor_tensor(out=T.t[:, sl], in0=C[:, c, sl], in1=self.bank[nb].t[:],
                                                              op=ALU.subtract),
                      reads=[Cb, self.bank[nb].b], writes=[T.b])
                kb.op(kb.DVE, lambda: nc.vector.tensor_tensor(out=T.t[:, sl], in0=T.t[:, sl],
                                                              in1=self.bank[4 + nb].t[:], op=ALU.mult),
                      reads=[self.bank[4 + nb].b], writes=[T.b])
            kb.op(kb.ACT, lambda: nc.scalar.activation(out=C[:, c, :], in_=T.t[:, 0:L], func=AF.Silu,
                                                       bias=self.pcol("ln_b", c), scale=self.pcol("ln_g", c)),
                  reads=[T.b, self.PV.b], writes=[Cb])
        self.linear_rmw(C, Cb, 16, [self.d_pw2[mb] for mb in range(16)], "b_pw2", xsrc, hdst, stats=True)

    def ffn(self, layer, h):
        kb = self.kb
        nc = self.nc
        AQ = self.RB[:, 0:11 * 2048].rearrange("p (c t) -> p c t", c=11)
        AQb = self.RBb
        for q in range(4):
            for j in range(11):
                m = q * 11 + j
                wg = self.load_w(self.d_wg[layer, m], 2048)
                wu = self.load_w(self.d_wu[layer, m], 2048)
                for w, b0 in ((wg, 0), (wu, 4)):
                    def mm():
                        for kc in range(16):
                            for nb in range(4):
                                ins = nc.tensor.matmul(self.bank[b0 + nb].t[:], w.t[:, kc * 128:(kc + 1) * 128],
                                                       self.HN[:, kc, nb * 512:(nb + 1) * 512],
                                                       start=(kc == 0), stop=(kc == 15))
                        return ins
                    kb.op(kb.PE, mm, reads=[w.b, self.HNb], writes=[self.bank[b0 + nb].b for nb in range(4)])
                    if b0 == 0:
                        sgs = []
                        for nb in range(4):
                            sg = self.SGQ[nb]
                            kb.op(kb.ACT, lambda: nc.scalar.activation(out=sg.t[:], in_=self.bank[nb].t[:],
                                                                       func=AF.Silu),
                                  reads=[self.bank[nb].b], writes=[sg.b])
                            sgs.append(sg)
                for nb in range(4):
                    sg = sgs[nb]
                    kb.op(kb.DVE, lambda: nc.vector.tensor_tensor(
                        out=AQ[:, j, nb * 512:(nb + 1) * 512], in0=self.bank[4 + nb].t[:], in1=sg.t[:], op=ALU.mult),
                        reads=[self.bank[4 + nb].b, sg.b], writes=[AQb])
            self.linear_rmw(AQ, AQb, 11, [self.d_wd[layer, q, mb] for mb in range(16)], None, h, h, stats=(q == 3))

    def filter_phase(self):
        kb = self.kb
        nc = self.nc
        FV, FW = self.FV, self.FW
        kb.dma(kb.SP, FV.d, FV.t[:, 0:6], self.d_fvec, writes=[FV.b])
        kb.dma(kb.SP, FW.d, FW.t[0:33, 0:64], self.d_fw1, writes=[FW.b])
        kb.dma(kb.SP, FW.d, FW.t[:, 64:128], self.d_fw2, writes=[FW.b])
        kb.dma(kb.SP, FW.d, FW.t[:, 128:192], self.d_fw3, writes=[FW.b])
        for k in range(3):
            kb.op(kb.DVE, lambda: nc.vector.tensor_tensor(out=FV.t[:, 2 * k:2 * k + 1], in0=FV.t[:, 2 * k:2 * k + 1],
                                                          in1=FV.t[:, 2 * k + 1:2 * k + 2], op=ALU.mult),
                  writes=[FV.b])
        Z = self.FS[0]
        kb.dma(kb.SP, Z.d, Z.t[0:33, 0:L], self.d_zT, writes=[Z.b])
        cur, curK = Z, 33
        HF = [self.FS[1], self.FS[2]]
        for k in range(3):
            nxt = HF[k % 2]
            wcol = (0, 64, 128)[k]
            for nb in range(4):
                sl = slice(nb * 512, (nb + 1) * 512)
                bk = self.bank[nb]
                kb.op(kb.PE, lambda: nc.tensor.matmul(bk.t[0:64, :], FW.t[0:curK, wcol:wcol + 64],
                                                      cur.t[0:curK, sl], start=True, stop=True),
                      reads=[FW.b, cur.b], writes=[bk.b])
                a = self.next_ss()
                kb.op(kb.DVE, lambda: nc.vector.tensor_scalar(out=a.t[0:64, :], in0=bk.t[0:64, :],
                                                              scalar1=FV.t[:, 2 * k + 1:2 * k + 2],
                                                              scalar2=FV.t[:, 2 * k:2 * k + 1],
                                                              op0=ALU.mult, op1=ALU.add),
                      reads=[bk.b, FV.b], writes=[a.b])
                kb.op(kb.DVE, lambda: nc.vector.tensor_scalar(out=a.t[0:64, :], in0=a.t[0:64, :],
                                                              scalar1=-TWO_PI, scalar2=TWO_PI,
                                                              op0=ALU.max, op1=ALU.min), writes=[a.b])
                s2 = self.next_ss()
                kb.op(kb.ACT, lambda: nc.scalar.activation(out=s2.t[0:64, :], in_=a.t[0:64, :], func=AF.Sin, scale=0.5),
                      reads=[a.b], writes=[s2.b])
                kb.op(kb.ACT, lambda: nc.scalar.activation(out=a.t[0:64, :], in_=a.t[0:64, :], func=AF.Sin, scale=0.25),
                      writes=[a.b])
                kb.op(kb.DVE, lambda: nc.vector.tensor_tensor(out=a.t[0:64, :], in0=a.t[0:64, :], in1=a.t[0:64, :],
                                                              op=ALU.mult), writes=[a.b])
                kb.op(kb.DVE, lambda: nc.vector.tensor_scalar(out=a.t[0:64, :], in0=a.t[0:64, :], scalar1=-4.0,
                                                              scalar2=2.0, op0=ALU.mult, op1=ALU.add), writes=[a.b])
                kb.op(kb.DVE, lambda: nc.vector.tensor_tensor(out=nxt.t[0:64, sl], in0=a.t[0:64, :], in1=s2.t[0:64, :],
                                                              op=ALU.mult), reads=[s2.b], writes=[a.b, nxt.b])
            cur, curK = nxt, 64
        HF3 = cur
        KT = self.arena[:, 0:32768].rearrange("p (s c) -> p s c", s=16)
        KTb = self.HNb
        tmp = self.arena[:, 32768:65536].bitcast(F32)
        DL = tmp[:, 0:2048]
        W4 = tmp[0:64, 2048:6144]
        DEC = tmp[:, 6144:8192]
        tb = self.RBb
        dd = self.new_ds
        kb.dma(kb.SP, dd, DL, self.d_delta, writes=[tb])
        kb.dma(kb.SP, dd, W4, self.d_fw4, writes=[tb])
        NKB = self.arena[:, 32768 + 16384:65536].rearrange("p (s c) -> p s c", s=16)
        DECS = [(tmp[:, 6144:7168], Buf()), (tmp[:, 7168:8192], Buf())]
        for dh in range(2):
            for sc in range(16):
                tsl = slice(128 * sc, 128 * sc + 128)
                DEC, DECb = DECS[sc % 2]
                kb.op(kb.ACT, lambda: nc.scalar.activation(out=DEC[:, 0:1024], in_=DL[:, dh * 1024:(dh + 1) * 1024],
                                                           func=AF.Exp, scale=self.pcol("negt", sc)),
                      reads=[tb, self.PV.b], writes=[DECb])
                for cb in range(4):
                    col0 = (cb // 2) * 2048 + dh * 1024 + (cb % 2) * 512
                    bk = self.bank[(sc * 4 + cb) % 8]
                    kb.op(kb.PE, lambda: nc.tensor.matmul(bk.t[:], HF3.t[0:64, tsl],
                                                          W4[:, col0:col0 + 512], start=True, stop=True),
                          reads=[HF3.b, tb], writes=[bk.b])
                    dsl = slice((cb % 2) * 512, (cb % 2) * 512 + 512)
                    kb.op(kb.DVE, lambda: nc.vector.tensor_tensor(
                        out=KT[:, sc, cb * 512:(cb + 1) * 512], in0=bk.t[:], in1=DEC[:, dsl], op=ALU.mult),
                        reads=[bk.b, DECb], writes=[KTb])
                    if cb >= 2:
                        kb.op(kb.DVE, lambda: nc.vector.scalar_tensor_tensor(
                            out=NKB[:, sc, dsl], in0=bk.t[:], scalar=-1.0, in1=DEC[:, dsl],
                            op0=ALU.mult, op1=ALU.mult), reads=[bk.b, DECb], writes=[KTb])
            kb.op(kb.DVE, lambda: nc.vector.memset(KT[0:1, 0, 1024:2048], 0.0), writes=[KTb])
            kb.op(kb.DVE, lambda: nc.vector.memset(NKB[0:1, 0, :], 0.0), writes=[KTb])
            for j in range(8):
                wc = self.load_w(self.d_cfh[j, 0], 2048, cast=False)
                ws = self.load_w(self.d_cfh[j, 1], 2048, cast=False)

                def grp(bk, w, par, second, secoff):
                    def mm():
                        for kc in range(8):
                            wl_ = w.t[:, par * 1024 + kc * 128:par * 1024 + (kc + 1) * 128]
                            nc.tensor.matmul(bk.t[:], wl_, KT[:, par * 8 + kc, qq * 512:qq * 512 + 512],
                                             start=(kc == 0), stop=False)
                            ins = nc.tensor.matmul(bk.t[:], wl_,
                                                   second[:, par * 8 + kc, secoff + qq * 512:secoff + qq * 512 + 512],
                                                   start=False, stop=(kc == 7))
                        return ins
                    kb.op(kb.PE, mm, reads=[w.b, KTb], writes=[bk.b])
                for qq in range(2):
                    grp(self.bank[qq * 4 + 0], wc, 0, KT, 1024)
                    grp(self.bank[qq * 4 + 1], wc, 1, KT, 1024)
                for qq in range(2):
                    grp(self.bank[qq * 4 + 2], ws, 0, NKB, 0)
                    grp(self.bank[qq * 4 + 3], ws, 1, NKB, 0)
                for qq in range(2):
                    q = dh * 2 + qq
                    bCe, bCo, bSe, bSo = [self.bank[qq * 4 + i] for i in range(4)]
                    kt = self.KFT[self.kfi % 2]
                    self.kfi += 1
                    c_, q_ = self.next_ss(), self.next_ss()
                    kb.op(kb.ACT, lambda: nc.scalar.activation(out=c_.t[:], in_=bCo.t[:], func=AF.Copy),
                          reads=[bCo.b], writes=[c_.b])
                    kb.op(kb.ACT, lambda: nc.scalar.activation(out=q_.t[:], in_=bSo.t[:], func=AF.Copy),
                          reads=[bSo.b], writes=[q_.b])
                    kb.op(kb.DVE, lambda: nc.vector.tensor_tensor(out=kt.t[:, 0:512], in0=bCe.t[:], in1=c_.t[:],
                                                                  op=ALU.add), reads=[bCe.b, c_.b], writes=[kt.b])
                    kb.op(kb.DVE, lambda: nc.vector.tensor_tensor(out=kt.t[:, 512:1024], in0=bCe.t[:], in1=c_.t[:],
                                                                  op=ALU.subtract), reads=[bCe.b, c_.b], writes=[kt.b])
                    kb.op(kb.DVE, lambda: nc.vector.tensor_tensor(out=kt.t[:, 1024:1536], in0=bSe.t[:], in1=q_.t[:],
                                                                  op=ALU.add), reads=[bSe.b, q_.b], writes=[kt.b])
                    kb.op(kb.DVE, lambda: nc.vector.tensor_tensor(out=kt.t[:, 1536:2048], in0=q_.t[:], in1=bSe.t[:],
                                                                  op=ALU.subtract), reads=[bSe.b, q_.b], writes=[kt.b])
                    kb.dma(kb.SP, kt.d, self.d_kf[q, j], kt.t[:], reads=[kt.b], writes=[self.kfb[q][j]])

    def hyena(self, h):
        kb = self.kb
        nc = self.nc
        RB = self.RB
        VT = RB[:, 0:8192].rearrange("p (s d) -> p s d", s=16)
        X0 = RB[:, 0:8192].rearrange("p (i t) -> p i t", i=4)
        VTb = Buf()
        VX = RB[:, 8192:16384].rearrange("p (i t) -> p i t", i=4)
        VXb = Buf()
        YF = RB[:, 16384:32768].rearrange("p (y f d) -> p y f d", y=4, f=8)
        YFb = Buf()

        B2 = self.B2
        c0, c1, c2 = PCOL["b_in"], PCOL["ws1"], PCOL["b_sh"]
        kb.op(kb.DVE, lambda: nc.vector.tensor_tensor(out=B2.t[:], in0=self.PV.t[:, c0:c0 + 48],
                                                      in1=self.PV.t[:, c1:c1 + 48], op=ALU.mult),
              reads=[self.PV.b], writes=[B2.b])
        kb.op(kb.DVE, lambda: nc.vector.tensor_tensor(out=B2.t[:], in0=B2.t[:], in1=self.PV.t[:, c2:c2 + 48],
                                                      op=ALU.add), reads=[self.PV.b], writes=[B2.b])

        def proj(mblk, O, Ob):
            w = self.load_w(self.d_win[mblk], 2048)
            b0 = (self.pj % 2) * 4
            ZC = self.FS[0] if (self.pj % 2 == 0) else self.FS[3]
            self.pj += 1

            def mm():
                for kc in range(16):
                    for nb in range(4):
                        ins = nc.tensor.matmul(self.bank[b0 + nb].t[:], w.t[:, kc * 128:(kc + 1) * 128],
                                               self.HN[:, kc, nb * 512:(nb + 1) * 512],
                                               start=(kc == 0), stop=(kc == 15))
                return ins
            kb.op(kb.PE, mm, reads=[w.b, self.HNb], writes=[self.bank[b0 + nb].b for nb in range(4)])
            kb.op(kb.DVE, lambda: nc.vector.memset(ZC.t[:, 0:1], 0.0), writes=[ZC.b])
            kb.op(kb.DVE, lambda: nc.vector.memset(ZC.t[:, 2049:2050], 0.0), writes=[ZC.b])
            for nb in range(4):
                bk = self.bank[b0 + nb]
                kb.op(kb.ACT, lambda: nc.scalar.activation(out=ZC.t[:, 1 + nb * 512:1 + (nb + 1) * 512],
                                                           in_=bk.t[:], func=AF.Identity,
                                                           bias=self.pcol("b_in", mblk), scale=1.0),
                      reads=[bk.b, self.PV.b], writes=[ZC.b])
                kb.op(kb.ACT, lambda: nc.scalar.activation(out=O[:, nb * 512:(nb + 1) * 512],
                                                           in_=bk.t[:], func=AF.Identity,
                                                           bias=B2.t[:, mblk:mblk + 1], scale=self.pcol("ws1", mblk)),
                      reads=[bk.b, self.PV.b, B2.b], writes=[Ob])
            return ZC

        def sconv(ZC, mblk, O, Ob, final_out=None, final_b=None):
            kb.op(kb.DVE, lambda: nc.vector.scalar_tensor_tensor(out=O[:, 0:L], in0=ZC.t[:, 0:L],
                                                                 scalar=self.pcol("ws0", mblk), in1=O[:, 0:L],
                                                                 op0=ALU.mult, op1=ALU.add),
                  reads=[ZC.b, self.PV.b], writes=[Ob])
            if final_out is None:
                kb.op(kb.DVE, lambda: nc.vector.scalar_tensor_tensor(out=O[:, 0:L], in0=ZC.t[:, 2:L + 2],
                                                                     scalar=self.pcol("ws2", mblk), in1=O[:, 0:L],
                                                                     op0=ALU.mult, op1=ALU.add),
                      reads=[ZC.b, self.PV.b], writes=[Ob])
            else:
                kb.op(kb.DVE, lambda: nc.vector.scalar_tensor_tensor(out=final_out, in0=ZC.t[:, 2:L + 2],
                                                                     scalar=self.pcol("ws2", mblk), in1=O[:, 0:L],
                                                                     op0=ALU.mult, op1=ALU.add),
                      reads=[ZC.b, self.PV.b, Ob], writes=[final_b])

        self.pj = 0
        OV, OX = self.FS[1], self.FS[2]
        for q in range(4):
            def transposes(i):
                for half in range(2):
                    bk = self.bank[(self.pj % 2) * 4 + half]
                    bkv = bk.t[:].bitcast(BF16)

                    def tr():
                        for k in range(8):
                            tc0 = (half * 8 + k) * 128
                            ins = nc.tensor.transpose(bkv[:, k * 128:(k + 1) * 128], VX[:, i, tc0:tc0 + 128],
                                                      self.IDENT.t[:])
                        return ins
                    kb.op(kb.PE, tr, reads=[VXb, self.IDENT.b], writes=[bk.b])
                    kb.op(kb.ACT, lambda: nc.scalar.activation(
                        out=VT[:, half * 8:(half + 1) * 8, i * 128:(i + 1) * 128],
                        in_=bkv.rearrange("p (k d) -> p k d", k=8), func=AF.Copy),
                        reads=[bk.b], writes=[VTb])

            for i in range(4):
                c = 4 * q + i
                ZC = proj(32 + c, OV.t, OV.b)
                sconv(ZC, 32 + c, OV.t, OV.b)
                ZC = proj(16 + c, OX.t, OX.b)
                if i > 0:
                    transposes(i - 1)
                sconv(ZC, 16 + c, OX.t, OX.b)
                kb.op(kb.DVE, lambda: nc.vector.tensor_tensor(out=VX[:, i, :], in0=OV.t[:, 0:L], in1=OX.t[:, 0:L],
                                                              op=ALU.mult), reads=[OV.b, OX.b], writes=[VXb])
            transposes(3)
            for j in range(8):
                kt = self.KFT[self.kfi % 2]
                self.kfi += 1
                kb.dma(kb.SP, kt.d, kt.t[:], self.d_kf[q, j], reads=[self.kfb[q][j]], writes=[kt.b])
                bs = (j % 2) * 4
                bAl, bBl, bAh, bBh = [self.bank[bs + i] for i in range(4)]
                for mblk, bk in ((j, bAl), (16 + j, bBl), (8 + j, bAh), (24 + j, bBh)):
                    w = self.load_w(self.d_cf[mblk], 2048, cast=False)

                    def mm():
                        for sc in range(16):
                            ins = nc.tensor.matmul(bk.t[:], w.t[:, sc * 128:(sc + 1) * 128], VT[:, sc, :],
                                                   start=(sc == 0), stop=(sc == 15))
                        return ins
                    kb.op(kb.PE, mm, reads=[w.b, VTb], writes=[bk.b])
                AKl, AKh, BKl, BKh = [kt.t[:, i * 512:(i + 1) * 512] for i in range(4)]

                def mul(bk, kk):
                    t = self.next_ss()
                    kb.op(kb.DVE, lambda: nc.vector.tensor_tensor(out=t.t[:], in0=bk.t[:], in1=kk, op=ALU.mult),
                          reads=[bk.b, kt.b], writes=[t.b])
                    return t

                def comb(a, b_, op, out=None, outb=None):
                    if out is None:
                        kb.op(kb.DVE, lambda: nc.vector.tensor_tensor(out=a.t[:], in0=a.t[:], in1=b_.t[:], op=op),
                              reads=[b_.b], writes=[a.b])
                    else:
                        kb.op(kb.DVE, lambda: nc.vector.tensor_tensor(out=out, in0=a.t[:], in1=b_.t[:], op=op),
                              reads=[a.b, b_.b], writes=[outb])
                t1, t2 = mul(bAl, AKl), mul(bBl, BKl)
                comb(t1, t2, ALU.subtract)
                t3, t4 = mul(bAh, AKh), mul(bBh, BKh)
                comb(t3, t4, ALU.subtract)
                comb(t1, t3, ALU.add, YF[:, 0, j, :], YFb)
                comb(t1, t3, ALU.subtract, YF[:, 2, j, :], YFb)
                t5, t6 = mul(bAl, BKl), mul(bBl, AKl)
                comb(t5, t6, ALU.add)
                t7, t8 = mul(bAh, BKh), mul(bBh, AKh)
                comb(t7, t8, ALU.add)
                comb(t5, t7, ALU.subtract, YF[:, 1, j, :], YFb)
                comb(t5, t7, ALU.add, YF[:, 3, j, :], YFb)
            for i in range(4):
                c = 4 * q + i
                ZC = proj(c, OV.t, OV.b)
                sconv(ZC, c, OV.t, OV.b, final_out=X0[:, i, :], final_b=VTb)
            CIS = [self.FS[1], self.FS[2]]
            cii = 0
            for par in range(2):
                for blk in range(2):
                    b0 = ((par * 2 + blk) % 2) * 4
                    for g in range(2):
                        cs = CIS[cii % 2]
                        cii += 1
                        civ = cs.t[:, 0:2048].bitcast(BF16).rearrange("p (k t) -> p k t", k=8)
                        kb.dma(kb.SP, cs.d, civ, self.d_ci[par, blk, g * 8:(g + 1) * 8].rearrange("k p t -> p k t"),
                               writes=[cs.b])

                        def mm():
                            for k in range(8):
                                for i in range(4):
                                    ins = nc.tensor.matmul(self.bank[b0 + i].t[:],
                                                           YF[:, 2 * par + g, k, i * 128:(i + 1) * 128],
                                                           civ[:, k, :], start=(g == 0 and k == 0),
                                                           stop=(g == 1 and k == 7))
                            return ins
                        kb.op(kb.PE, mm, reads=[cs.b, YFb], writes=[self.bank[b0 + i].b for i in range(4)])
                    for i in range(4):
                        c = 4 * q + i
                        sl = slice(1024 * blk + par, 1024 * blk + 1024, 2)
                        t = self.next_ss()
                        kb.op(kb.DVE, lambda: nc.vector.scalar_tensor_tensor(
                            out=t.t[:], in0=VX[:, i, sl], scalar=self.pcol("skip", c), in1=self.bank[b0 + i].t[:],
                            op0=ALU.mult, op1=ALU.add), reads=[VXb, self.bank[b0 + i].b, self.PV.b], writes=[t.b])
                        kb.op(kb.DVE, lambda: nc.vector.tensor_tensor(out=VX[:, i, sl], in0=t.t[:], in1=X0[:, i, sl],
                                                                      op=ALU.mult), reads=[t.b, VTb], writes=[VXb])
            kb.dma(kb.SP, self.gds, self.d_g[4 * q:4 * q + 4].rearrange("c p t -> p c t"), VX,
                   reads=[VXb], writes=[self.gb[q]])
        kb.barrier()
        G = self.HN
        for q in range(4):
            kb.dma(kb.SP, self.gds, G[:, 4 * q:4 * q + 4, :], self.d_g[4 * q:4 * q + 4].rearrange("c p t -> p c t"),
                   reads=[self.gb[q]], writes=[self.HNb])
        self.linear_rmw(G, self.HNb, 16, [self.d_wout[mb] for mb in range(16)], "b_out", h, h, stats=True)


def _patch_prog():
    orig_emit = Prog.emit

    def emit(self):
        nc = self.nc
        kb = self.kb
        self.SGQ = self.SS[0:4]
        ce = nc.alloc_sbuf_tensor("epsn", [128, 2], F32)
        self.EPSN = ce[:, 0:1]
        self.EPSL = ce[:, 1:2]
        self.new_ds = kb.new_dsem("misc")
        self.gds = kb.new_dsem("gscr")
        eb = Buf()
        kb.op(kb.DVE, lambda: nc.vector.memset(ce[:, 0:1], NORM_EPS), writes=[eb])
        kb.op(kb.DVE, lambda: nc.vector.memset(ce[:, 1:2], LN_EPS), writes=[eb])
        kb.barrier()
        orig_emit(self)
    Prog.emit = emit


_patch_prog()


def _cc(v):
    v = np.asarray(v, np.float32)
    return np.ascontiguousarray(v.reshape(-1, 128).T)


def _wl(W):
    W = np.asarray(W, np.float32)
    K, M = W.shape
    kc, mb = K // 128, M // 128
    return np.ascontiguousarray(W.reshape(kc, 128, mb, 128).transpose(2, 1, 0, 3).reshape(mb, 128, kc * 128))


_CONST = {}


def _constants():
    if _CONST:
        return _CONST
    f32 = np.float32
    N = 2 * L
    s = np.arange(L, dtype=np.int64)
    g = np.arange(L, dtype=np.int64)
    f = np.where(g < 1024, g, L - 1 - (g - 1024))
    w0 = 2.0 * np.pi / (2 * N)
    ph = ((2 * f[None, :] + 1) * s[:, None]) % (2 * N)
    ang = ph.astype(np.float64) * w0
    CF = np.concatenate([np.cos(ang), np.sin(ang)], axis=1)
    cf = np.ascontiguousarray(CF.reshape(16, 128, 32, 128).transpose(2, 1, 0, 3).reshape(32, 128, 2048)).astype(NPBF)
    kc_ = np.arange(8, dtype=np.int64)[:, None, None]
    p_ = np.arange(128, dtype=np.int64)[None, :, None]
    q_ = np.arange(128, dtype=np.int64)[None, None, :]
    cfh = np.zeros((8, 2, 128, 2048), np.float64)
    for j in range(8):
        for par in range(2):
            sv = 2 * (128 * kc_ + p_) + par
            a = (((2 * (128 * j + q_) + 1) * sv) % (2 * N)).astype(np.float64) * w0
            cfh[j, 0, :, par * 1024:(par + 1) * 1024] = np.cos(a).transpose(1, 0, 2).reshape(128, 1024)
            cfh[j, 1, :, par * 1024:(par + 1) * 1024] = np.sin(a).transpose(1, 0, 2).reshape(128, 1024)
    cfh = cfh.astype(NPBF)
    ci = np.zeros((2, 2, 16, 128, 512), np.float64)
    r_ = np.arange(8, dtype=np.int64)[:, None, None]
    m_ = np.arange(512, dtype=np.int64)[None, None, :]
    for par in range(2):
        for blk in range(2):
            tv = 2 * (512 * blk + m_) + par
            a = (((2 * (128 * r_ + p_) + 1) * tv) % (2 * N)).astype(np.float64) * w0
            ci[par, blk, 0:8] = np.cos(a) * (2.0 / N)
            ci[par, blk, 8:16] = np.sin(a) * (2.0 / N)
    ci = ci.astype(NPBF)
    t = np.linspace(0.0, 1.0, L, dtype=f32)[:, None]
    bands = np.linspace(1e-4, 15.0, 16, dtype=f32)[None, :]
    wpos = (f32(2.0 * math.pi) * np.arange(L, dtype=f32)[:, None] / f32(L)).astype(f32)
    z = np.concatenate([t, np.cos(bands * wpos), -np.sin(bands * wpos)], axis=-1).astype(f32)
    max_decay = math.log(1e-2) / 0.3
    min_decay = math.log(1e-2) / 1.5
    deltas = np.abs(np.linspace(min_decay, max_decay, D, dtype=f32)).astype(f32)
    _CONST.update(dict(
        cf=cf, ci=ci, cfh=cfh, zT=np.ascontiguousarray(np.concatenate([z[0::2], z[1::2]], axis=0).T), negt=_cc(-t[:, 0].reshape(1024, 2).T.reshape(-1)),
        delta=np.ascontiguousarray(np.broadcast_to(deltas[None, :], (128, D))).astype(f32),
        ident=np.eye(128, dtype=f32).astype(NPBF), ones=np.ones((128, 128), f32).astype(NPBF)))
    return _CONST


def _prep_shared(inp):
    g = lambda k: np.asarray(inp[k], np.float32)
    C = _constants()
    pv = np.zeros((128, NPV), np.float32)

    def put(name, arr):
        pv[:, PCOL[name]:PCOL[name] + arr.shape[1]] = arr
    nm, nf = g("norm_mix"), g("norm_ffn")
    put("nm0", _cc(nm[0])); put("nm1", _cc(nm[1])); put("nf0", _cc(nf[0])); put("nf1", _cc(nf[1]))
    put("nfin", _cc(g("norm_final")))
    put("b_pw1", _cc(g("cv_b_pw1")[0])); put("b_dw", _cc(g("cv_b_dw")[0]))
    put("ln_g", _cc(g("cv_ln_g")[0])); put("ln_b", _cc(g("cv_ln_b")[0])); put("b_pw2", _cc(g("cv_b_pw2")[0]))
    put("b_in", _cc(g("hy_b_in")[0])); put("b_sh", _cc(g("hy_b_short")[0]))
    wsh = g("hy_w_short")[0]
    put("ws0", _cc(wsh[0])); put("ws1", _cc(wsh[1])); put("ws2", _cc(wsh[2]))
    put("skip", _cc(g("hy_skip")[0])); put("b_out", _cc(g("hy_b_out")[0]))
    wdw = g("cv_w_dw")[0]
    wd = wdw.T.reshape(16, 128, 31).transpose(1, 0, 2).reshape(128, 496)
    put("wdw", np.ascontiguousarray(wd))
    put("negt", C["negt"])
    wdn = g("ffn_w_down")
    w_down = np.stack([np.stack([_wl(wdn[l][q * 1408:(q + 1) * 1408]) for q in range(4)]) for l in range(2)])
    wo = g("hy_w_out")[0]
    w_out = _wl(wo)
    fvec = np.stack([g("hy_f_b1")[0], g("hy_f_freq1")[0], g("hy_f_b2")[0], g("hy_f_freq2")[0],
                     g("hy_f_b3")[0], g("hy_f_freq3")[0]], axis=1).astype(np.float32)
    sh = dict(
        pv=pv, w_pw1=_wl(g("cv_w_pw1")[0]), w_pw2=_wl(g("cv_w_pw2")[0]),
        w_gate=np.stack([_wl(g("ffn_w_gate")[l]) for l in range(2)]),
        w_up=np.stack([_wl(g("ffn_w_up")[l]) for l in range(2)]),
        w_down=w_down, w_in=_wl(g("hy_w_in")[0]), w_out=w_out,
        fw1=np.ascontiguousarray(g("hy_f_w1")[0]), fw2=np.ascontiguousarray(g("hy_f_w2")[0]),
        fw3=np.ascontiguousarray(g("hy_f_w3")[0]), fw4=np.ascontiguousarray(g("hy_f_w4")[0]),
        fvec=np.ascontiguousarray(fvec), zT=C["zT"], delta=C["delta"], cf=C["cf"], ci=C["ci"], cfh=C["cfh"],
        ident=C["ident"], ones=C["ones"])
    return sh


ALL_PHASES = ("filter", "conf", "ffn0", "hyena", "ffn1", "final")
_PROG_CACHE = {}


def run_phases(inp, phases, extra=None):
    key = tuple(phases)
    if key not in _PROG_CACHE:
        _PROG_CACHE[key] = Prog(phases)
    prog = _PROG_CACHE[key]
    sh = _prep_shared(inp)
    x = np.asarray(inp["x"], np.float32)
    in_maps = []
    for b in range(8):
        m = dict(sh)
        m["xT"] = np.ascontiguousarray(x[b].T.reshape(16, 128, L))
        if extra:
            for k, v in extra.items():
                m[k] = v[b]
        in_maps.append(m)
    res = run_bass_kernel_spmd(prog.nc, in_maps, core_ids=list(range(8)))
    return res.results


def kernel(**inputs):
    res = run_phases(inputs, ALL_PHASES)
    out = np.empty((8, L, D), np.float32)
    for b in range(8):
        out[b] = res[b]["outT"].reshape(D, L).T
    return out
```

```python
import math
import numpy as np
import ml_dtypes
import concourse.bass as bass
import concourse.mybir as mybir
from concourse.bass_utils import run_bass_kernel_spmd

F32 = mybir.dt.float32
BF16 = mybir.dt.bfloat16
AF = mybir.ActivationFunctionType
ALU = mybir.AluOpType
NPBF = ml_dtypes.bfloat16

D = 2048
L = 2048
FF = 5632
NCH = 16
NORM_EPS = 1e-6
LN_EPS = 1e-5
TWO_PI = 2.0 * math.pi

PCOL = {}


def _mkcols():
    off = 0
    for name, n in [("nm0", 16), ("nf0", 16), ("nm1", 16), ("nf1", 16), ("nfin", 16),
                    ("b_pw1", 32), ("b_dw", 16), ("ln_g", 16), ("ln_b", 16), ("b_pw2", 16),
                    ("b_in", 48), ("b_sh", 48), ("ws0", 48), ("ws1", 48), ("ws2", 48),
                    ("skip", 16), ("b_out", 16), ("wdw", 496), ("negt", 16)]:
        PCOL[name] = off
        off += n
    return off


NPV = _mkcols()


class Buf:
    __slots__ = ("w", "r")

    def __init__(self):
        self.w = None
        self.r = {}


class Eng:
    def __init__(self, nc, eng, name, is_pe=False):
        self.e = eng
        self.sem = nc.alloc_semaphore("se_" + name)
        self.cnt = 0
        self.seen = {}
        self.is_pe = is_pe


class DSem:
    def __init__(self, nc, name):
        self.sem = nc.alloc_semaphore("sd_" + name)
        self.cnt = 0


class Slot:
    def __init__(self, nc, name, shape, dtype, dsem=True):
        self.t = nc.alloc_sbuf_tensor(name, shape, dtype)
        self.b = Buf()
        self.d = DSem(nc, name) if dsem else None


class KB:
    def __init__(self, nc):
        self.nc = nc
        self.PE = Eng(nc, nc.tensor, "pe", True)
        self.ACT = Eng(nc, nc.scalar, "act")
        self.DVE = Eng(nc, nc.vector, "dve")
        self.POOL = Eng(nc, nc.gpsimd, "pool")
        self.SP = Eng(nc, nc.sync, "sp")
        self.engs = [self.PE, self.ACT, self.DVE, self.POOL, self.SP]
        self.dsems = []

    def wait(self, E, ev):
        sem, val = ev
        if E.is_pe and sem is E.sem:
            return
        if E.seen.get(sem.num, 0) >= val:
            return
        E.e.wait_ge(sem, val)
        E.seen[sem.num] = val

    def _deps(self, E, reads, writes):
        for b in reads:
            if b.w is not None:
                self.wait(E, b.w)
        for b in writes:
            if b.w is not None:
                self.wait(E, b.w)
            for ev in b.r.values():
                self.wait(E, ev)

    def _commit(self, ev, reads, writes):
        for b in reads:
            b.r[ev[0].num] = ev
        for b in writes:
            b.w = ev
            b.r = {}

    def op(self, E, fn, reads=(), writes=()):
        self._deps(E, reads, writes)
        ins = fn()
        E.cnt += 1
        ins.then_inc(E.sem, 1)
        self._commit((E.sem, E.cnt), reads, writes)

    def dma(self, Q, ds, out, in_, reads=(), writes=(), **kw):
        self._deps(Q, reads, writes)
        if ds.cnt > 0:
            self.wait(Q, (ds.sem, ds.cnt))
        ins = Q.e.dma_start(out=out, in_=in_, **kw)
        ds.cnt += 16
        ins.then_inc(ds.sem, 16)
        self._commit((ds.sem, ds.cnt), reads, writes)

    def new_dsem(self, name):
        d = DSem(self.nc, name)
        self.dsems.append(d)
        return d

    def barrier(self):
        for E in self.engs:
            for E2 in self.engs:
                if E2 is not E and E2.cnt > 0:
                    self.wait(E, (E2.sem, E2.cnt))
            for d in self.dsems:
                if d.cnt > 0:
                    self.wait(E, (d.sem, d.cnt))

    def finish(self):
        E = self.SP
        for E2 in self.engs:
            if E2 is not E and E2.cnt > 0:
                self.wait(E, (E2.sem, E2.cnt))
        for d in self.dsems:
            if d.cnt > 0:
                self.wait(E, (d.sem, d.cnt))


class Prog:
    def __init__(self, phases, dbg=False):
        nc = bass.Bass("TRN2", target_bir_lowering=False)
        self.nc = nc
        self.kb = KB(nc)
        kb = self.kb
        self.phases = phases

        def din(name, shape, dt=F32):
            return nc.dram_tensor(name, shape, dt, kind="ExternalInput").ap()

        self.d_x = din("xT", [16, 128, L])
        self.d_pv = din("pv", [128, NPV])
        self.d_pw1 = din("w_pw1", [32, 128, 2048])
        self.d_pw2 = din("w_pw2", [16, 128, 2048])
        self.d_wg = din("w_gate", [2, 44, 128, 2048])
        self.d_wu = din("w_up", [2, 44, 128, 2048])
        self.d_wd = din("w_down", [2, 4, 16, 128, 1408])
        self.d_win = din("w_in", [48, 128, 2048])
        self.d_wout = din("w_out", [16, 128, 2048])
        self.d_fw1 = din("fw1", [33, 64])
        self.d_fw2 = din("fw2", [64, 64])
        self.d_fw3 = din("fw3", [64, 64])
        self.d_fw4 = din("fw4", [64, 4096])
        self.d_fvec = din("fvec", [64, 6])
        self.d_zT = din("zT", [33, L])
        self.d_delta = din("delta", [128, D])
        self.d_cf = din("cf", [32, 128, 2048], BF16)
        self.d_cfh = din("cfh", [8, 2, 128, 2048], BF16)
        self.d_ci = din("ci", [2, 2, 16, 128, 512], BF16)
        self.d_ident = din("ident", [128, 128], BF16)
        self.d_ones = din("ones", [128, 128], BF16)
        hk = "ExternalInput" if ("hin" in phases) else "Internal"
        self.d_h = nc.dram_tensor("h", [16, 128, L], F32, kind=hk).ap()
        self.d_kf = nc.dram_tensor("kf", [4, 8, 128, 2048], BF16,
                                   kind=("ExternalInput" if "kfin" in phases else "Internal")).ap()
        self.d_out = nc.dram_tensor("outT", [16, 128, L], F32, kind="ExternalOutput").ap()
        self.d_g = nc.dram_tensor("gscr", [16, 128, L], BF16).ap()
        self.gb = [Buf() for _ in range(4)]
        self.d_kfo = None
        if "kfout" in phases:
            self.d_kfo = nc.dram_tensor("kfo", [4, 8, 128, 2048], BF16, kind="ExternalOutput").ap()

        self.hb = [Buf() for _ in range(16)]
        self.kfb = [[Buf() for _ in range(16)] for _ in range(4)]

        self.arena = nc.alloc_sbuf_tensor("arena", [128, 65536], BF16)
        self.arena_b = Buf()
        self.HN = self.arena[:, 0:32768].rearrange("p (c t) -> p c t", c=16)
        self.HNb = Buf()
        self.RB = self.arena[:, 32768:65536]
        self.RBb = Buf()
        self.FS = [Slot(nc, f"fs{i}", [128, 2050], F32) for i in range(4)]
        self.W = [Slot(nc, f"w{i}", [128, 2048], BF16) for i in range(4)]
        self.SS = [Slot(nc, f"ss{i}", [128, 512], F32, dsem=False) for i in range(8)]
        self.KFT = [Slot(nc, f"kft{i}", [128, 2048], BF16) for i in range(2)]
        self.PV = Slot(nc, "pvt", [128, NPV], F32)
        self.IDENT = Slot(nc, "identt", [128, 128], BF16)
        self.ONES = Slot(nc, "onest", [128, 128], BF16)
        self.FV = Slot(nc, "fvect", [64, 8], F32)
        self.B2 = Slot(nc, "b2t", [128, 48], F32, dsem=False)
        self.FW = Slot(nc, "fwt", [64, 192], F32)
        for s in self.FS + self.W + self.KFT + [self.PV, self.IDENT, self.ONES, self.FV, self.FW]:
            kb.dsems.append(s.d)
        self.bank = []
        for i in range(8):
            s = Slot.__new__(Slot)
            s.t = nc.alloc_psum_tensor(f"bank{i}", [128, 512], F32)
            s.b = Buf()
            s.d = None
            self.bank.append(s)
        self.wi = 0
        self.ftiles = {}
        self.fsi = 0
        self.ssi = 0
        self.kfi = 0

        self.emit()

    def pcol(self, name, c, n=128):
        return self.PV.t[0:n, PCOL[name] + c:PCOL[name] + c + 1]

    def load_w(self, src, ncols, cast=True):
        kb = self.kb
        s = self.W[self.wi % len(self.W)]
        self.wi += 1
        Q = kb.POOL if cast else kb.SP
        kb.dma(Q, s.d, s.t[:, 0:ncols], src, writes=[s.b])
        return s

    def next_fs(self):
        s = self.FS[self.fsi % len(self.FS)]
        self.fsi += 1
        return s

    def next_ss(self):
        s = self.SS[self.ssi % len(self.SS)]
        self.ssi += 1
        return s

    def emit(self):
        kb = self.kb
        nc = self.nc
        ph = self.phases
        kb.dma(kb.SP, self.PV.d, self.PV.t[:], self.d_pv, writes=[self.PV.b])
        kb.dma(kb.SP, self.IDENT.d, self.IDENT.t[:], self.d_ident, writes=[self.IDENT.b])
        kb.dma(kb.SP, self.ONES.d, self.ONES.t[:], self.d_ones, writes=[self.ONES.b])
        xsrc = [(self.d_x[c], None) for c in range(16)]
        hch = [(self.d_h[c], self.hb[c]) for c in range(16)]
        och = [(self.d_out[c], Buf()) for c in range(16)]
        if "filter" in ph:
            self.filter_phase()
            kb.barrier()
        if "conf" in ph:
            self.rmsnorm(xsrc, "nm0", None)
            self.conformer(xsrc, hch)
            kb.barrier()
        if "ffn0" in ph:
            self.rmsnorm(hch, "nf0", None, have_stats=("conf" in ph))
            self.ffn(0, hch)
            kb.barrier()
        if "hyena" in ph:
            self.rmsnorm(hch, "nm1", None, have_stats=("ffn0" in ph))
            self.hyena(hch)
            kb.barrier()
        if "ffn1" in ph:
            self.rmsnorm(hch, "nf1", None, have_stats=("hyena" in ph))
            self.ffn(1, hch)
            kb.barrier()
        if "final" in ph:
            self.rmsnorm(hch, "nfin", och, have_stats=("ffn1" in ph))
        else:
            for c in range(16):
                s = self.next_fs()
                kb.dma(kb.SP, s.d, s.t[:, 0:L], hch[c][0], reads=[hch[c][1]], writes=[s.b])
                kb.dma(kb.SP, s.d, och[c][0], s.t[:, 0:L], reads=[s.b], writes=[och[c][1]])
        if self.d_kfo is not None:
            for q in range(4):
                for fc in range(8):
                    s = self.KFT[(q * 8 + fc) % 2]
                    kb.dma(kb.SP, s.d, s.t[:], self.d_kf[q, fc], reads=[self.kfb[q][fc]], writes=[s.b])
                    kb.dma(kb.SP, s.d, self.d_kfo[q, fc], s.t[:], reads=[s.b])
        kb.finish()

    def rmsnorm(self, src, gname, dst, have_stats=False):
        kb = self.kb
        nc = self.nc
        sb = 4 if have_stats else 0
        SQ = [(self.FS[3].t[:, 0:1024].bitcast(BF16), self.FS[3].b)]
        if not have_stats:
            for c in range(16):
                ap, b = src[c]
                xs = self.FS[c % 3]
                kb.dma(kb.SP, xs.d, xs.t[:, 0:L], ap, reads=([b] if b else []), writes=[xs.b])
                sq, sqb = SQ[0]
                kb.op(kb.ACT, lambda: nc.scalar.activation(out=sq, in_=xs.t[:, 0:L], func=AF.Square),
                      reads=[xs.b], writes=[sqb])

                def mm():
                    for nb in range(4):
                        ins = nc.tensor.matmul(self.bank[nb].t[:], self.ONES.t[:], sq[:, nb * 512:(nb + 1) * 512],
                                               start=(c == 0), stop=(c == 15))
                    return ins
                kb.op(kb.PE, mm, reads=[sqb, self.ONES.b], writes=[self.bank[nb].b for nb in range(4)])
        for nb in range(4):
            bk = self.bank[sb + nb]
            tmp = self.next_ss()
            kb.op(kb.ACT, lambda: nc.scalar.activation(out=tmp.t[:], in_=bk.t[:], func=AF.Sqrt,
                                                       bias=self.EPSN[:, 0:1], scale=1.0 / D),
                  reads=[bk.b], writes=[tmp.b])
            kb.op(kb.DVE, lambda: nc.vector.reciprocal(out=bk.t[:], in_=tmp.t[:]),
                  reads=[tmp.b], writes=[bk.b])
        def ld2(m):
            ap, b = src[m]
            x_ = self.FS[m % 3]
            kb.dma(kb.SP, x_.d, x_.t[:, 0:L], ap, reads=([b] if b else []), writes=[x_.b])
        ld2(0)
        for c in range(16):
            xs = self.FS[c % 3]
            if c + 1 < 16:
                ld2(c + 1)
            for nb in range(4):
                sl = slice(nb * 512, (nb + 1) * 512)
                bk = self.bank[sb + nb]
                if dst is None:
                    kb.op(kb.DVE, lambda: nc.vector.scalar_tensor_tensor(
                        out=self.HN[:, c, sl], in0=xs.t[:, sl], scalar=self.pcol(gname, c), in1=bk.t[:],
                        op0=ALU.mult, op1=ALU.mult), reads=[xs.b, bk.b, self.PV.b], writes=[self.HNb])
                else:
                    kb.op(kb.DVE, lambda: nc.vector.scalar_tensor_tensor(
                        out=xs.t[:, sl], in0=xs.t[:, sl], scalar=self.pcol(gname, c), in1=bk.t[:],
                        op0=ALU.mult, op1=ALU.mult), reads=[bk.b, self.PV.b], writes=[xs.b])
            if dst is not None:
                kb.dma(kb.SP, xs.d, dst[c][0], xs.t[:, 0:L], reads=[xs.b], writes=[dst[c][1]])

    def linear_rmw(self, act, actb, KC, wt, bias, src, dst, stats=False):
        kb = self.kb
        nc = self.nc
        SQ = self.FS[3].t[:, 0:1024].bitcast(BF16)
        SQb = self.FS[3].b
        pend = None

        def do_stats(mbp, xsp):
            kb.op(kb.ACT, lambda: nc.scalar.activation(out=SQ, in_=xsp.t[:, 0:L], func=AF.Square),
                  reads=[xsp.b], writes=[SQb])

            def mms():
                for nb in range(4):
                    ins = nc.tensor.matmul(self.bank[4 + nb].t[:], self.ONES.t[:], SQ[:, nb * 512:(nb + 1) * 512],
                                           start=(mbp == 0), stop=(mbp == 15))
                return ins
            kb.op(kb.PE, mms, reads=[SQb, self.ONES.b], writes=[self.bank[4 + nb].b for nb in range(4)])

        def ld(m):
            ap, b = src[m]
            x_ = self.FS[m % 3]
            kb.dma(kb.SP, x_.d, x_.t[:, 0:L], ap, reads=([b] if b else []), writes=[x_.b])
        ld(0)
        for mb in range(16):
            w = self.load_w(wt[mb], KC * 128)
            xs = self.FS[mb % 3]
            if mb + 1 < 16:
                ld(mb + 1)
            ngrp = 2 if stats else 1
            npg = 4 // ngrp
            for gi in range(ngrp):
                if stats:
                    bl = [self.bank[gi * 2 + k] for k in range(2)]
                else:
                    bl = [self.bank[(mb % 2) * 4 + k] for k in range(4)]

                def mm():
                    for kc in range(KC):
                        for k in range(npg):
                            nb = gi * npg + k
                            ins = nc.tensor.matmul(bl[k].t[:], w.t[:, kc * 128:(kc + 1) * 128],
                                                   act[:, kc, nb * 512:(nb + 1) * 512],
                                                   start=(kc == 0), stop=(kc == KC - 1))
                    return ins
                kb.op(kb.PE, mm, reads=[w.b, actb], writes=[bk.b for bk in bl])
                if stats and gi == 0 and pend is not None:
                    do_stats(*pend)
                    pend = None
                for k in range(npg):
                    nb = gi * npg + k
                    sl = slice(nb * 512, (nb + 1) * 512)
                    if bias is not None:
                        kb.op(kb.DVE, lambda: nc.vector.scalar_tensor_tensor(
                            out=xs.t[:, sl], in0=bl[k].t[:], scalar=self.pcol(bias, mb), in1=xs.t[:, sl],
                            op0=ALU.add, op1=ALU.add), reads=[bl[k].b, self.PV.b], writes=[xs.b])
                    else:
                        kb.op(kb.DVE, lambda: nc.vector.tensor_tensor(
                            out=xs.t[:, sl], in0=bl[k].t[:], in1=xs.t[:, sl], op=ALU.add),
                            reads=[bl[k].b], writes=[xs.b])
            kb.dma(kb.SP, xs.d, dst[mb][0], xs.t[:, 0:L], reads=[xs.b], writes=[dst[mb][1]])
            if stats:
                pend = (mb, xs)
        if pend is not None:
            do_stats(*pend)

    def conformer(self, xsrc, hdst):
        kb = self.kb
        nc = self.nc
        C = self.RB.rearrange("p (c t) -> p c t", c=16)
        Cb = self.RBb
        DG = self.FS[0].t[:, 0:1984].bitcast(BF16).rearrange("p (j q) -> p j q", j=31)
        DGb = self.FS[0].b
        U = [(self.FS[1 + i].t[:, 0:1039].bitcast(BF16), self.FS[1 + i].b) for i in range(2)]
        for u, ub in U:
            kb.op(kb.DVE, lambda: nc.vector.memset(u, 0.0), writes=[ub])
        prev = None
        for c in range(17):
            if prev is not None:
                def dgb():
                    for j in range(31):
                        ins = nc.vector.tensor_scalar(out=DG[:, j, :], in0=self.IDENT.t[:],
                                                      scalar1=self.pcol("wdw", prev[0] * 31 + j), scalar2=None,
                                                      op0=ALU.mult)
                    return ins
                kb.op(kb.DVE, dgb, reads=[self.IDENT.b, self.PV.b], writes=[DGb])
            if c < 16:
                wv = self.load_w(self.d_pw1[c], 2048)
                wg = self.load_w(self.d_pw1[16 + c], 2048)
                u, ub = U[c % 2]
                for half in range(2):
                    bs = 4 * half

                    def mm():
                        for kc in range(16):
                            for wt, boff in ((wv, 0), (wg, 2)):
                                for nb in range(2):
                                    t0 = half * 1024 + nb * 512
                                    ins = nc.tensor.matmul(self.bank[bs + boff + nb].t[:],
                                                           wt.t[:, kc * 128:(kc + 1) * 128],
                                                           self.HN[:, kc, t0:t0 + 512],
                                                           start=(kc == 0), stop=(kc == 15))
                        return ins
                    kb.op(kb.PE, mm, reads=[wv.b, wg.b, self.HNb], writes=[self.bank[bs + i].b for i in range(4)])
                    for nb in range(2):
                        t0 = 15 + half * 1024 + nb * 512
                        sg = self.next_ss()
                        gb = self.bank[bs + 2 + nb]
                        vb = self.bank[bs + nb]
                        kb.op(kb.ACT, lambda: nc.scalar.activation(out=sg.t[:], in_=gb.t[:], func=AF.Sigmoid,
                                                                   bias=self.pcol("b_pw1", 16 + c), scale=1.0),
                              reads=[gb.b, self.PV.b], writes=[sg.b])
                        kb.op(kb.DVE, lambda: nc.vector.scalar_tensor_tensor(
                            out=u[:, t0:t0 + 512], in0=vb.t[:], scalar=self.pcol("b_pw1", c), in1=sg.t[:],
                            op0=ALU.add, op1=ALU.mult), reads=[vb.b, sg.b, self.PV.b], writes=[ub])
            if prev is not None:
                pc, pu, pub = prev
                for nb in range(4):
                    bk = self.bank[nb]

                    def mmc():
                        for j in range(31):
                            ins = nc.tensor.matmul(bk.t[:], DG[:, j, :], pu[:, nb * 512 + j:nb * 512 + j + 512],
                                                   start=(j == 0), stop=(j == 30))
                        return ins
                    kb.op(kb.PE, mmc, reads=[DGb, pub], writes=[bk.b])
                    kb.op(kb.ACT, lambda: nc.scalar.activation(out=C[:, pc, nb * 512:(nb + 1) * 512], in_=bk.t[:],
                                                               func=AF.Identity, bias=self.pcol("b_dw", pc), scale=1.0),
                          reads=[bk.b, self.PV.b], writes=[Cb])
            if c < 16:
                prev = (c, u, ub)
        kb.barrier()
        SQS = [(self.FS[3].t[:, 0:1024].bitcast(BF16), self.FS[3].b),
               (self.FS[2].t[:, 0:1024].bitcast(BF16), self.FS[2].b)]
        for c in range(16):
            SQ, SQb = SQS[c % 2]
            kb.op(kb.ACT, lambda: nc.scalar.activation(out=SQ, in_=C[:, c, :], func=AF.Square),
                  reads=[Cb], writes=[SQb])

            def mm():
                for nb in range(4):
                    nc.tensor.matmul(self.bank[nb].t[:], self.ONES.t[:], C[:, c, nb * 512:(nb + 1) * 512],
                                     start=(c == 0), stop=(c == 15))
                    ins = nc.tensor.matmul(self.bank[4 + nb].t[:], self.ONES.t[:], SQ[:, nb * 512:(nb + 1) * 512],
                                           start=(c == 0), stop=(c == 15))
                return ins
            kb.op(kb.PE, mm, reads=[Cb, SQb, self.ONES.b], writes=[bk.b for bk in self.bank])
        T1, T2 = self.FS[2], self.FS[3]
        for nb in range(4):
            bm, bs_ = self.bank[nb], self.bank[4 + nb]
            kb.op(kb.ACT, lambda: nc.scalar.activation(out=bm.t[:], in_=bm.t[:], func=AF.Copy, scale=1.0 / D),
                  writes=[bm.b])
            msq = self.next_ss()
            kb.op(kb.ACT, lambda: nc.scalar.activation(out=msq.t[:], in_=bm.t[:], func=AF.Square),
                  reads=[bm.b], writes=[msq.b])
            var = self.next_ss()
            kb.op(kb.DVE, lambda: nc.vector.scalar_tensor_tensor(
                out=var.t[:], in0=bs_.t[:], scalar=1.0 / D, in1=msq.t[:],
                op0=ALU.mult, op1=ALU.subtract), reads=[bs_.b, msq.b], writes=[var.b])
            kb.op(kb.ACT, lambda: nc.scalar.activation(out=var.t[:], in_=var.t[:], func=AF.Sqrt,
                                                       bias=self.EPSL[:, 0:1], scale=1.0), writes=[var.b])
            kb.op(kb.DVE, lambda: nc.vector.reciprocal(out=bs_.t[:], in_=var.t[:]), reads=[var.b], writes=[bs_.b])
        for c in range(16):
            T = T1 if c % 2 == 0 else T2
            for nb in range(4):
                sl = slice(nb * 512, (nb + 1) * 512)
                kb.op(kb.DVE, lambda: nc.vector.tensor_tensor(out=T.t[:, sl], in0=C[:, c, sl], in1=self.bank[nb].t[:],
                                                              op=ALU.subtract),
                      reads=[Cb, self.bank[nb].b], writes=[T.b])
                kb.op(kb.DVE, lambda: nc.vector.tensor_tensor(out=T.t[:, sl], in0=T.t[:, sl],
                                                              in1=self.bank[4 + nb].t[:], op=ALU.mult),
                      reads=[self.bank[4 + nb].b], writes=[T.b])
            kb.op(kb.ACT, lambda: nc.scalar.activation(out=C[:, c, :], in_=T.t[:, 0:L], func=AF.Silu,
                                                       bias=self.pcol("ln_b", c), scale=self.pcol("ln_g", c)),
                  reads=[T.b, self.PV.b], writes=[Cb])
        self.linear_rmw(C, Cb, 16, [self.d_pw2[mb] for mb in range(16)], "b_pw2", xsrc, hdst, stats=True)

    def ffn(self, layer, h):
        kb = self.kb
        nc = self.nc
        AQ = self.RB[:, 0:11 * 2048].rearrange("p (c t) -> p c t", c=11)
        AQb = self.RBb
        for q in range(4):
            for j in range(11):
                m = q * 11 + j
                wg = self.load_w(self.d_wg[layer, m], 2048)
                wu = self.load_w(self.d_wu[layer, m], 2048)
                for w, b0 in ((wg, 0), (wu, 4)):
                    def mm():
                        for kc in range(16):
                            for nb in range(4):
                                ins = nc.tensor.matmul(self.bank[b0 + nb].t[:], w.t[:, kc * 128:(kc + 1) * 128],
                                                       self.HN[:, kc, nb * 512:(nb + 1) * 512],
                                                       start=(kc == 0), stop=(kc == 15))
                        return ins
                    kb.op(kb.PE, mm, reads=[w.b, self.HNb], writes=[self.bank[b0 + nb].b for nb in range(4)])
                    if b0 == 0:
                        sgs = []
                        for nb in range(4):
                            sg = self.SGQ[nb]
                            kb.op(kb.ACT, lambda: nc.scalar.activation(out=sg.t[:], in_=self.bank[nb].t[:],
                                                                       func=AF.Silu),
                                  reads=[self.bank[nb].b], writes=[sg.b])
                            sgs.append(sg)
                for nb in range(4):
                    sg = sgs[nb]
                    kb.op(kb.DVE, lambda: nc.vector.tensor_tensor(
                        out=AQ[:, j, nb * 512:(nb + 1) * 512], in0=self.bank[4 + nb].t[:], in1=sg.t[:], op=ALU.mult),
                        reads=[self.bank[4 + nb].b, sg.b], writes=[AQb])
            self.linear_rmw(AQ, AQb, 11, [self.d_wd[layer, q, mb] for mb in range(16)], None, h, h, stats=(q == 3))

    def filter_phase(self):
        kb = self.kb
        nc = self.nc
        FV, FW = self.FV, self.FW
        kb.dma(kb.SP, FV.d, FV.t[:, 0:6], self.d_fvec, writes=[FV.b])
        kb.dma(kb.SP, FW.d, FW.t[0:33, 0:64], self.d_fw1, writes=[FW.b])
        kb.dma(kb.SP, FW.d, FW.t[:, 64:128], self.d_fw2, writes=[FW.b])
        kb.dma(kb.SP, FW.d, FW.t[:, 128:192], self.d_fw3, writes=[FW.b])
        for k in range(3):
            kb.op(kb.DVE, lambda: nc.vector.tensor_tensor(out=FV.t[:, 2 * k:2 * k + 1], in0=FV.t[:, 2 * k:2 * k + 1],
                                                          in1=FV.t[:, 2 * k + 1:2 * k + 2], op=ALU.mult),
                  writes=[FV.b])
        Z = self.FS[0]
        kb.dma(kb.SP, Z.d, Z.t[0:33, 0:L], self.d_zT, writes=[Z.b])
        cur, curK = Z, 33
        HF = [self.FS[1], self.FS[2]]
        for k in range(3):
            nxt = HF[k % 2]
            wcol = (0, 64, 128)[k]
            for nb in range(4):
                sl = slice(nb * 512, (nb + 1) * 512)
                bk = self.bank[nb]
                kb.op(kb.PE, lambda: nc.tensor.matmul(bk.t[0:64, :], FW.t[0:curK, wcol:wcol + 64],
                                                      cur.t[0:curK, sl], start=True, stop=True),
                      reads=[FW.b, cur.b], writes=[bk.b])
                a = self.next_ss()
                kb.op(kb.DVE, lambda: nc.vector.tensor_scalar(out=a.t[0:64, :], in0=bk.t[0:64, :],
                                                              scalar1=FV.t[:, 2 * k + 1:2 * k + 2],
                                                              scalar2=FV.t[:, 2 * k:2 * k + 1],
                                                              op0=ALU.mult, op1=ALU.add),
                      reads=[bk.b, FV.b], writes=[a.b])
                kb.op(kb.DVE, lambda: nc.vector.tensor_scalar(out=a.t[0:64, :], in0=a.t[0:64, :],
                                                              scalar1=-TWO_PI, scalar2=TWO_PI,
                                                              op0=ALU.max, op1=ALU.min), writes=[a.b])
                s2 = self.next_ss()
                kb.op(kb.ACT, lambda: nc.scalar.activation(out=s2.t[0:64, :], in_=a.t[0:64, :], func=AF.Sin, scale=0.5),
                      reads=[a.b], writes=[s2.b])
                kb.op(kb.ACT, lambda: nc.scalar.activation(out=a.t[0:64, :], in_=a.t[0:64, :], func=AF.Sin, scale=0.25),
                      writes=[a.b])
                kb.op(kb.DVE, lambda: nc.vector.tensor_tensor(out=a.t[0:64, :], in0=a.t[0:64, :], in1=a.t[0:64, :],
                                                              op=ALU.mult), writes=[a.b])
                kb.op(kb.DVE, lambda: nc.vector.tensor_scalar(out=a.t[0:64, :], in0=a.t[0:64, :], scalar1=-4.0,
                                                              scalar2=2.0, op0=ALU.mult, op1=ALU.add), writes=[a.b])
                kb.op(kb.DVE, lambda: nc.vector.tensor_tensor(out=nxt.t[0:64, sl], in0=a.t[0:64, :], in1=s2.t[0:64, :],
                                                              op=ALU.mult), reads=[s2.b], writes=[a.b, nxt.b])
            cur, curK = nxt, 64
        HF3 = cur
        KT = self.arena[:, 0:32768].rearrange("p (s c) -> p s c", s=16)
        KTb = self.HNb
        tmp = self.arena[:, 32768:65536].bitcast(F32)
        DL = tmp[:, 0:2048]
        W4 = tmp[0:64, 2048:6144]
        DEC = tmp[:, 6144:8192]
        tb = self.RBb
        dd = self.new_ds
        kb.dma(kb.SP, dd, DL, self.d_delta, writes=[tb])
        kb.dma(kb.SP, dd, W4, self.d_fw4, writes=[tb])
        NKB = self.arena[:, 32768 + 16384:65536].rearrange("p (s c) -> p s c", s=16)
        DECS = [(tmp[:, 6144:7168], Buf()), (tmp[:, 7168:8192], Buf())]
        for dh in range(2):
            for sc in range(16):
                tsl = slice(128 * sc, 128 * sc + 128)
                DEC, DECb = DECS[sc % 2]
                kb.op(kb.ACT, lambda: nc.scalar.activation(out=DEC[:, 0:1024], in_=DL[:, dh * 1024:(dh + 1) * 1024],
                                                           func=AF.Exp, scale=self.pcol("negt", sc)),
                      reads=[tb, self.PV.b], writes=[DECb])
                for cb in range(4):
                    col0 = (cb // 2) * 2048 + dh * 1024 + (cb % 2) * 512
                    bk = self.bank[(sc * 4 + cb) % 8]
                    kb.op(kb.PE, lambda: nc.tensor.matmul(bk.t[:], HF3.t[0:64, tsl],
                                                          W4[:, col0:col0 + 512], start=True, stop=True),
                          reads=[HF3.b, tb], writes=[bk.b])
                    dsl = slice((cb % 2) * 512, (cb % 2) * 512 + 512)
                    kb.op(kb.DVE, lambda: nc.vector.tensor_tensor(
                        out=KT[:, sc, cb * 512:(cb + 1) * 512], in0=bk.t[:], in1=DEC[:, dsl], op=ALU.mult),
                        reads=[bk.b, DECb], writes=[KTb])
                    if cb >= 2:
                        kb.op(kb.DVE, lambda: nc.vector.scalar_tensor_tensor(
                            out=NKB[:, sc, dsl], in0=bk.t[:], scalar=-1.0, in1=DEC[:, dsl],
                            op0=ALU.mult, op1=ALU.mult), reads=[bk.b, DECb], writes=[KTb])
            kb.op(kb.DVE, lambda: nc.vector.memset(KT[0:1, 0, 1024:2048], 0.0), writes=[KTb])
            kb.op(kb.DVE, lambda: nc.vector.memset(NKB[0:1, 0, :], 0.0), writes=[KTb])
            for j in range(8):
                for n_ in (dh * 8 + j, dh * 8 + j + 1):
                    if n_ < 16 and n_ not in self.ftiles:
                        self.ftiles[n_] = (self.load_w(self.d_cfh[n_ % 8, 0], 2048, cast=False),
                                           self.load_w(self.d_cfh[n_ % 8, 1], 2048, cast=False))
                wc, ws = self.ftiles[dh * 8 + j]

                def grp(bk, w, par, second, secoff):
                    def mm():
                        for kc in range(8):
                            wl_ = w.t[:, par * 1024 + kc * 128:par * 1024 + (kc + 1) * 128]
                            nc.tensor.matmul(bk.t[:], wl_, KT[:, par * 8 + kc, qq * 512:qq * 512 + 512],
                                             start=(kc == 0), stop=False)
                            ins = nc.tensor.matmul(bk.t[:], wl_,
                                                   second[:, par * 8 + kc, secoff + qq * 512:secoff + qq * 512 + 512],
                                                   start=False, stop=(kc == 7))
                        return ins
                    kb.op(kb.PE, mm, reads=[w.b, KTb], writes=[bk.b])
                for qq in range(2):
                    grp(self.bank[qq * 4 + 0], wc, 0, KT, 1024)
                    grp(self.bank[qq * 4 + 1], wc, 1, KT, 1024)
                for qq in range(2):
                    grp(self.bank[qq * 4 + 2], ws, 0, NKB, 0)
                    grp(self.bank[qq * 4 + 3], ws, 1, NKB, 0)
                for qq in range(2):
                    q = dh * 2 + qq
                    bCe, bCo, bSe, bSo = [self.bank[qq * 4 + i] for i in range(4)]
                    kt = self.KFT[self.kfi % 2]
                    self.kfi += 1
                    c_, q_ = self.next_ss(), self.next_ss()
                    kb.op(kb.ACT, lambda: nc.scalar.activation(out=c_.t[:], in_=bCo.t[:], func=AF.Copy),
                          reads=[bCo.b], writes=[c_.b])
                    kb.op(kb.ACT, lambda: nc.scalar.activation(out=q_.t[:], in_=bSo.t[:], func=AF.Copy),
                          reads=[bSo.b], writes=[q_.b])
                    kb.op(kb.DVE, lambda: nc.vector.tensor_tensor(out=kt.t[:, 0:512], in0=bCe.t[:], in1=c_.t[:],
                                                                  op=ALU.add), reads=[bCe.b, c_.b], writes=[kt.b])
                    kb.op(kb.DVE, lambda: nc.vector.tensor_tensor(out=kt.t[:, 512:1024], in0=bCe.t[:], in1=c_.t[:],
                                                                  op=ALU.subtract), reads=[bCe.b, c_.b], writes=[kt.b])
                    kb.op(kb.DVE, lambda: nc.vector.tensor_tensor(out=kt.t[:, 1024:1536], in0=bSe.t[:], in1=q_.t[:],
                                                                  op=ALU.add), reads=[bSe.b, q_.b], writes=[kt.b])
                    kb.op(kb.DVE, lambda: nc.vector.tensor_tensor(out=kt.t[:, 1536:2048], in0=q_.t[:], in1=bSe.t[:],
                                                                  op=ALU.subtract), reads=[bSe.b, q_.b], writes=[kt.b])
                    kb.dma(kb.SP, kt.d, self.d_kf[q, j], kt.t[:], reads=[kt.b], writes=[self.kfb[q][j]])

    def hyena(self, h):
        kb = self.kb
        nc = self.nc
        RB = self.RB
        VT = RB[:, 0:8192].rearrange("p (s d) -> p s d", s=16)
        X0 = RB[:, 0:8192].rearrange("p (i t) -> p i t", i=4)
        VTb = Buf()
        VX = RB[:, 8192:16384].rearrange("p (i t) -> p i t", i=4)
        VXb = Buf()
        YF = RB[:, 16384:32768].rearrange("p (y f d) -> p y f d", y=4, f=8)
        YFb = Buf()

        B2 = self.B2
        c0, c1, c2 = PCOL["b_in"], PCOL["ws1"], PCOL["b_sh"]
        kb.op(kb.DVE, lambda: nc.vector.tensor_tensor(out=B2.t[:], in0=self.PV.t[:, c0:c0 + 48],
                                                      in1=self.PV.t[:, c1:c1 + 48], op=ALU.mult),
              reads=[self.PV.b], writes=[B2.b])
        kb.op(kb.DVE, lambda: nc.vector.tensor_tensor(out=B2.t[:], in0=B2.t[:], in1=self.PV.t[:, c2:c2 + 48],
                                                      op=ALU.add), reads=[self.PV.b], writes=[B2.b])

        def proj(mblk, O, Ob):
            w = self.load_w(self.d_win[mblk], 2048)
            b0 = (self.pj % 2) * 4
            ZC = self.FS[0] if (self.pj % 2 == 0) else self.FS[3]
            self.pj += 1

            def mm():
                for kc in range(16):
                    for nb in range(4):
                        ins = nc.tensor.matmul(self.bank[b0 + nb].t[:], w.t[:, kc * 128:(kc + 1) * 128],
                                               self.HN[:, kc, nb * 512:(nb + 1) * 512],
                                               start=(kc == 0), stop=(kc == 15))
                return ins
            kb.op(kb.PE, mm, reads=[w.b, self.HNb], writes=[self.bank[b0 + nb].b for nb in range(4)])
            kb.op(kb.DVE, lambda: nc.vector.memset(ZC.t[:, 0:1], 0.0), writes=[ZC.b])
            kb.op(kb.DVE, lambda: nc.vector.memset(ZC.t[:, 2049:2050], 0.0), writes=[ZC.b])
            for nb in range(4):
                bk = self.bank[b0 + nb]
                kb.op(kb.ACT, lambda: nc.scalar.activation(out=ZC.t[:, 1 + nb * 512:1 + (nb + 1) * 512],
                                                           in_=bk.t[:], func=AF.Identity,
                                                           bias=self.pcol("b_in", mblk), scale=1.0),
                      reads=[bk.b, self.PV.b], writes=[ZC.b])
                kb.op(kb.ACT, lambda: nc.scalar.activation(out=O[:, nb * 512:(nb + 1) * 512],
                                                           in_=bk.t[:], func=AF.Identity,
                                                           bias=B2.t[:, mblk:mblk + 1], scale=self.pcol("ws1", mblk)),
                      reads=[bk.b, self.PV.b, B2.b], writes=[Ob])
            return ZC

        def sconv(ZC, mblk, O, Ob, final_out=None, final_b=None):
            kb.op(kb.DVE, lambda: nc.vector.scalar_tensor_tensor(out=O[:, 0:L], in0=ZC.t[:, 0:L],
                                                                 scalar=self.pcol("ws0", mblk), in1=O[:, 0:L],
                                                                 op0=ALU.mult, op1=ALU.add),
                  reads=[ZC.b, self.PV.b], writes=[Ob])
            if final_out is None:
                kb.op(kb.DVE, lambda: nc.vector.scalar_tensor_tensor(out=O[:, 0:L], in0=ZC.t[:, 2:L + 2],
                                                                     scalar=self.pcol("ws2", mblk), in1=O[:, 0:L],
                                                                     op0=ALU.mult, op1=ALU.add),
                      reads=[ZC.b, self.PV.b], writes=[Ob])
            else:
                kb.op(kb.DVE, lambda: nc.vector.scalar_tensor_tensor(out=final_out, in0=ZC.t[:, 2:L + 2],
                                                                     scalar=self.pcol("ws2", mblk), in1=O[:, 0:L],
                                                                     op0=ALU.mult, op1=ALU.add),
                      reads=[ZC.b, self.PV.b, Ob], writes=[final_b])

        self.pj = 0
        OV, OX = self.FS[1], self.FS[2]
        for q in range(4):
            def transposes(i):
                for half in range(2):
                    bk = self.bank[(self.pj % 2) * 4 + half]
                    bkv = bk.t[:].bitcast(BF16)

                    def tr():
                        for k in range(8):
                            tc0 = (half * 8 + k) * 128
                            ins = nc.tensor.transpose(bkv[:, k * 128:(k + 1) * 128], VX[:, i, tc0:tc0 + 128],
                                                      self.IDENT.t[:])
                        return ins
                    kb.op(kb.PE, tr, reads=[VXb, self.IDENT.b], writes=[bk.b])
                    kb.op(kb.DVE, lambda: nc.vector.tensor_copy(
                        out=VT[:, half * 8:(half + 1) * 8, i * 128:(i + 1) * 128],
                        in_=bkv.rearrange("p (k d) -> p k d", k=8)),
                        reads=[bk.b], writes=[VTb])

            for i in range(4):
                c = 4 * q + i
                ZC = proj(32 + c, OV.t, OV.b)
                if i > 0:
                    transposes(i - 1)
                sconv(ZC, 32 + c, OV.t, OV.b)
                ZC = proj(16 + c, OX.t, OX.b)
                sconv(ZC, 16 + c, OX.t, OX.b)
                kb.op(kb.DVE, lambda: nc.vector.tensor_tensor(out=VX[:, i, :], in0=OV.t[:, 0:L], in1=OX.t[:, 0:L],
                                                              op=ALU.mult), reads=[OV.b, OX.b], writes=[VXb])
            transposes(3)
            for j in range(8):
                kt = self.KFT[self.kfi % 2]
                self.kfi += 1
                kb.dma(kb.SP, kt.d, kt.t[:], self.d_kf[q, j], reads=[self.kfb[q][j]], writes=[kt.b])
                bs = (j % 2) * 4
                bAl, bBl, bAh, bBh = [self.bank[bs + i] for i in range(4)]
                for mblk, bk in ((j, bAl), (16 + j, bBl), (8 + j, bAh), (24 + j, bBh)):
                    w = self.load_w(self.d_cf[mblk], 2048, cast=False)

                    def mm():
                        for sc in range(16):
                            ins = nc.tensor.matmul(bk.t[:], w.t[:, sc * 128:(sc + 1) * 128], VT[:, sc, :],
                                                   start=(sc == 0), stop=(sc == 15))
                        return ins
                    kb.op(kb.PE, mm, reads=[w.b, VTb], writes=[bk.b])
                AKl, AKh, BKl, BKh = [kt.t[:, i * 512:(i + 1) * 512] for i in range(4)]

                def mul(bk, kk):
                    t = self.next_ss()
                    kb.op(kb.DVE, lambda: nc.vector.tensor_tensor(out=t.t[:], in0=bk.t[:], in1=kk, op=ALU.mult),
                          reads=[bk.b, kt.b], writes=[t.b])
                    return t

                def comb(a, b_, op, out=None, outb=None):
                    if out is None:
                        kb.op(kb.DVE, lambda: nc.vector.tensor_tensor(out=a.t[:], in0=a.t[:], in1=b_.t[:], op=op),
                              reads=[b_.b], writes=[a.b])
                    else:
                        kb.op(kb.DVE, lambda: nc.vector.tensor_tensor(out=out, in0=a.t[:], in1=b_.t[:], op=op),
                              reads=[a.b, b_.b], writes=[outb])
                t1, t2 = mul(bAl, AKl), mul(bBl, BKl)
                comb(t1, t2, ALU.subtract)
                t3, t4 = mul(bAh, AKh), mul(bBh, BKh)
                comb(t3, t4, ALU.subtract)
                comb(t1, t3, ALU.add, YF[:, 0, j, :], YFb)
                comb(t1, t3, ALU.subtract, YF[:, 2, j, :], YFb)
                t5, t6 = mul(bAl, BKl), mul(bBl, AKl)
                comb(t5, t6, ALU.add)
                t7, t8 = mul(bAh, BKh), mul(bBh, AKh)
                comb(t7, t8, ALU.add)
                comb(t5, t7, ALU.subtract, YF[:, 1, j, :], YFb)
                comb(t5, t7, ALU.add, YF[:, 3, j, :], YFb)
            for i in range(4):
                c = 4 * q + i
                ZC = proj(c, OV.t, OV.b)
                sconv(ZC, c, OV.t, OV.b, final_out=X0[:, i, :], final_b=VTb)
            CIS = [self.FS[2], self.FS[0]]
            cii = 0
            for par in range(2):
                for blk in range(2):
                    b0 = ((par * 2 + blk) % 2) * 4
                    for g in range(2):
                        cs = CIS[cii % 2]
                        cii += 1
                        civ = cs.t[:, 0:2048].bitcast(BF16).rearrange("p (k t) -> p k t", k=8)
                        kb.dma(kb.SP, cs.d, civ, self.d_ci[par, blk, g * 8:(g + 1) * 8].rearrange("k p t -> p k t"),
                               writes=[cs.b])

                        def mm():
                            for k in range(8):
                                for i in range(4):
                                    ins = nc.tensor.matmul(self.bank[b0 + i].t[:],
                                                           YF[:, 2 * par + g, k, i * 128:(i + 1) * 128],
                                                           civ[:, k, :], start=(g == 0 and k == 0),
                                                           stop=(g == 1 and k == 7))
                            return ins
                        kb.op(kb.PE, mm, reads=[cs.b, YFb], writes=[self.bank[b0 + i].b for i in range(4)])
                    for i in range(4):
                        c = 4 * q + i
                        sl = slice(1024 * blk + par, 1024 * blk + 1024, 2)
                        t = self.next_ss()
                        kb.op(kb.DVE, lambda: nc.vector.scalar_tensor_tensor(
                            out=t.t[:], in0=VX[:, i, sl], scalar=self.pcol("skip", c), in1=self.bank[b0 + i].t[:],
                            op0=ALU.mult, op1=ALU.add), reads=[VXb, self.bank[b0 + i].b, self.PV.b], writes=[t.b])
                        kb.op(kb.DVE, lambda: nc.vector.tensor_tensor(out=VX[:, i, sl], in0=t.t[:], in1=X0[:, i, sl],
                                                                      op=ALU.mult), reads=[t.b, VTb], writes=[VXb])
            kb.dma(kb.SP, self.gds, self.d_g[4 * q:4 * q + 4].rearrange("c p t -> p c t"), VX,
                   reads=[VXb], writes=[self.gb[q]])
        kb.barrier()
        G = self.HN
        for q in range(4):
            kb.dma(kb.SP, self.gds, G[:, 4 * q:4 * q + 4, :], self.d_g[4 * q:4 * q + 4].rearrange("c p t -> p c t"),
                   reads=[self.gb[q]], writes=[self.HNb])
        self.linear_rmw(G, self.HNb, 16, [self.d_wout[mb] for mb in range(16)], "b_out", h, h, stats=True)


def _patch_prog():
    orig_emit = Prog.emit

    def emit(self):
        nc = self.nc
        kb = self.kb
        self.SGQ = self.SS[0:4]
        ce = nc.alloc_sbuf_tensor("epsn", [128, 2], F32)
        self.EPSN = ce[:, 0:1]
        self.EPSL = ce[:, 1:2]
        self.new_ds = kb.new_dsem("misc")
        self.gds = kb.new_dsem("gscr")
        eb = Buf()
        kb.op(kb.DVE, lambda: nc.vector.memset(ce[:, 0:1], NORM_EPS), writes=[eb])
        kb.op(kb.DVE, lambda: nc.vector.memset(ce[:, 1:2], LN_EPS), writes=[eb])
        kb.barrier()
        orig_emit(self)
    Prog.emit = emit


_patch_prog()


def _cc(v):
    v = np.asarray(v, np.float32)
    return np.ascontiguousarray(v.reshape(-1, 128).T)


def _wl(W):
    W = np.asarray(W, np.float32)
    K, M = W.shape
    kc, mb = K // 128, M // 128
    return np.ascontiguousarray(W.reshape(kc, 128, mb, 128).transpose(2, 1, 0, 3).reshape(mb, 128, kc * 128))


_CONST = {}


def _constants():
    if _CONST:
        return _CONST
    f32 = np.float32
    N = 2 * L
    s = np.arange(L, dtype=np.int64)
    g = np.arange(L, dtype=np.int64)
    f = np.where(g < 1024, g, L - 1 - (g - 1024))
    w0 = 2.0 * np.pi / (2 * N)
    ph = ((2 * f[None, :] + 1) * s[:, None]) % (2 * N)
    ang = ph.astype(np.float64) * w0
    CF = np.concatenate([np.cos(ang), np.sin(ang)], axis=1)
    cf = np.ascontiguousarray(CF.reshape(16, 128, 32, 128).transpose(2, 1, 0, 3).reshape(32, 128, 2048)).astype(NPBF)
    kc_ = np.arange(8, dtype=np.int64)[:, None, None]
    p_ = np.arange(128, dtype=np.int64)[None, :, None]
    q_ = np.arange(128, dtype=np.int64)[None, None, :]
    cfh = np.zeros((8, 2, 128, 2048), np.float64)
    for j in range(8):
        for par in range(2):
            sv = 2 * (128 * kc_ + p_) + par
            a = (((2 * (128 * j + q_) + 1) * sv) % (2 * N)).astype(np.float64) * w0
            cfh[j, 0, :, par * 1024:(par + 1) * 1024] = np.cos(a).transpose(1, 0, 2).reshape(128, 1024)
            cfh[j, 1, :, par * 1024:(par + 1) * 1024] = np.sin(a).transpose(1, 0, 2).reshape(128, 1024)
    cfh = cfh.astype(NPBF)
    ci = np.zeros((2, 2, 16, 128, 512), np.float64)
    r_ = np.arange(8, dtype=np.int64)[:, None, None]
    m_ = np.arange(512, dtype=np.int64)[None, None, :]
    for par in range(2):
        for blk in range(2):
            tv = 2 * (512 * blk + m_) + par
            a = (((2 * (128 * r_ + p_) + 1) * tv) % (2 * N)).astype(np.float64) * w0
            ci[par, blk, 0:8] = np.cos(a) * (2.0 / N)
            ci[par, blk, 8:16] = np.sin(a) * (2.0 / N)
    ci = ci.astype(NPBF)
    t = np.linspace(0.0, 1.0, L, dtype=f32)[:, None]
    bands = np.linspace(1e-4, 15.0, 16, dtype=f32)[None, :]
    wpos = (f32(2.0 * math.pi) * np.arange(L, dtype=f32)[:, None] / f32(L)).astype(f32)
    z = np.concatenate([t, np.cos(bands * wpos), -np.sin(bands * wpos)], axis=-1).astype(f32)
    max_decay = math.log(1e-2) / 0.3
    min_decay = math.log(1e-2) / 1.5
    deltas = np.abs(np.linspace(min_decay, max_decay, D, dtype=f32)).astype(f32)
    _CONST.update(dict(
        cf=cf, ci=ci, cfh=cfh, zT=np.ascontiguousarray(np.concatenate([z[0::2], z[1::2]], axis=0).T), negt=_cc(-t[:, 0].reshape(1024, 2).T.reshape(-1)),
        delta=np.ascontiguousarray(np.broadcast_to(deltas[None, :], (128, D))).astype(f32),
        ident=np.eye(128, dtype=f32).astype(NPBF), ones=np.ones((128, 128), f32).astype(NPBF)))
    return _CONST


def _prep_shared(inp):
    g = lambda k: np.asarray(inp[k], np.float32)
    C = _constants()
    pv = np.zeros((128, NPV), np.float32)

    def put(name, arr):
        pv[:, PCOL[name]:PCOL[name] + arr.shape[1]] = arr
    nm, nf = g("norm_mix"), g("norm_ffn")
    put("nm0", _cc(nm[0])); put("nm1", _cc(nm[1])); put("nf0", _cc(nf[0])); put("nf1", _cc(nf[1]))
    put("nfin", _cc(g("norm_final")))
    put("b_pw1", _cc(g("cv_b_pw1")[0])); put("b_dw", _cc(g("cv_b_dw")[0]))
    put("ln_g", _cc(g("cv_ln_g")[0])); put("ln_b", _cc(g("cv_ln_b")[0])); put("b_pw2", _cc(g("cv_b_pw2")[0]))
    put("b_in", _cc(g("hy_b_in")[0])); put("b_sh", _cc(g("hy_b_short")[0]))
    wsh = g("hy_w_short")[0]
    put("ws0", _cc(wsh[0])); put("ws1", _cc(wsh[1])); put("ws2", _cc(wsh[2]))
    put("skip", _cc(g("hy_skip")[0])); put("b_out", _cc(g("hy_b_out")[0]))
    wdw = g("cv_w_dw")[0]
    wd = wdw.T.reshape(16, 128, 31).transpose(1, 0, 2).reshape(128, 496)
    put("wdw", np.ascontiguousarray(wd))
    put("negt", C["negt"])
    wdn = g("ffn_w_down")
    w_down = np.stack([np.stack([_wl(wdn[l][q * 1408:(q + 1) * 1408]) for q in range(4)]) for l in range(2)])
    wo = g("hy_w_out")[0]
    w_out = _wl(wo)
    fvec = np.stack([g("hy_f_b1")[0], g("hy_f_freq1")[0], g("hy_f_b2")[0], g("hy_f_freq2")[0],
                     g("hy_f_b3")[0], g("hy_f_freq3")[0]], axis=1).astype(np.float32)
    sh = dict(
        pv=pv, w_pw1=_wl(g("cv_w_pw1")[0]), w_pw2=_wl(g("cv_w_pw2")[0]),
        w_gate=np.stack([_wl(g("ffn_w_gate")[l]) for l in range(2)]),
        w_up=np.stack([_wl(g("ffn_w_up")[l]) for l in range(2)]),
        w_down=w_down, w_in=_wl(g("hy_w_in")[0]), w_out=w_out,
        fw1=np.ascontiguousarray(g("hy_f_w1")[0]), fw2=np.ascontiguousarray(g("hy_f_w2")[0]),
        fw3=np.ascontiguousarray(g("hy_f_w3")[0]), fw4=np.ascontiguousarray(g("hy_f_w4")[0]),
        fvec=np.ascontiguousarray(fvec), zT=C["zT"], delta=C["delta"], cf=C["cf"], ci=C["ci"], cfh=C["cfh"],
        ident=C["ident"], ones=C["ones"])
    return sh


ALL_PHASES = ("filter", "conf", "ffn0", "hyena", "ffn1", "final")
_PROG_CACHE = {}


def run_phases(inp, phases, extra=None):
    key = tuple(phases)
    if key not in _PROG_CACHE:
        _PROG_CACHE[key] = Prog(phases)
    prog = _PROG_CACHE[key]
    sh = _prep_shared(inp)
    x = np.asarray(inp["x"], np.float32)
    in_maps = []
    for b in range(8):
        m = dict(sh)
        m["xT"] = np.ascontiguousarray(x[b].T.reshape(16, 128, L))
        if extra:
            for k, v in extra.items():
                m[k] = v[b]
        in_maps.append(m)
    res = run_bass_kernel_spmd(prog.nc, in_maps, core_ids=list(range(8)))
    return res.results


def kernel(**inputs):
    res = run_phases(inputs, ALL_PHASES)
    out = np.empty((8, L, D), np.float32)
    for b in range(8):
        out[b] = res[b]["outT"].reshape(D, L).T
    return out
```
